# Optimizing a Trainium2 kernel written in Bass

```python
import jax, jax.numpy as jnp
from jax import lax
import numpy as np

D_MODEL = 1024
BATCH = 2
SEQ = 8192
DEPTH = 4

N_MIXERS = 3
EPS = 1e-6
CHUNK = 128
A_WIDTH = 2 * D_MODEL
A_GROUPS = 8
A_GROUP_DIM = A_WIDTH // A_GROUPS
HEAD_DIM = 128
B_HEADS = D_MODEL // HEAD_DIM
B_PATTERNS = ((128, 1), (512, 4), (2048, 16))
N_B_GROUPS = len(B_PATTERNS)
B_WIDTH = B_HEADS * HEAD_DIM
B_IN_WIDTH = 3 * N_B_GROUPS * B_WIDTH + B_WIDTH
ROPE_DIM = HEAD_DIM // 4
ROPE_THETA = 500000.0
POOL_SIZES = (2, 4, 8, 16)
N_POOL = len(POOL_SIZES)
C_WIDTH = 2 * D_MODEL
C_GROUP = C_WIDTH // N_POOL
N_A = (DEPTH + 2) // 3
N_B = (DEPTH + 1) // 3
N_C = DEPTH // 3

kernel_name = "hybrid_gmlp_dilated_attn_pool_interleaved"


def rms_norm(x, g):
    xf = x.astype(jnp.float32)
    y = xf * lax.rsqrt(jnp.mean(xf * xf, axis=-1, keepdims=True) + EPS)
    return (y * g.astype(jnp.float32)).astype(x.dtype)


def rotary_tables(seq_len):
    half = ROPE_DIM // 2
    inv_freq = jnp.power(jnp.float32(ROPE_THETA), -jnp.arange(half, dtype=jnp.float32) / half)
    ang = jnp.arange(seq_len, dtype=jnp.float32)[:, None] * inv_freq[None, :]
    return jnp.cos(ang)[None, :, None, :], jnp.sin(ang)[None, :, None, :]


def apply_partial_rotary(x, cos, sin):
    half = ROPE_DIM // 2
    x1 = x[..., :half].astype(jnp.float32)
    x2 = x[..., half:ROPE_DIM].astype(jnp.float32)
    rot = jnp.concatenate([x1 * cos - x2 * sin, x2 * cos + x1 * sin], axis=-1)
    return jnp.concatenate([rot.astype(x.dtype), x[..., ROPE_DIM:]], axis=-1)


def dilated_window_attention(q, k, v, span, dilation):
    bsz, S, H, hd = q.shape
    blk = span
    L = S // dilation
    nb = -(-L // blk)
    Lp = nb * blk

    def to_blocks(t):
        t = t.reshape(bsz, L, dilation, H, hd).transpose(0, 2, 1, 3, 4)
        t = jnp.pad(t, ((0, 0), (0, 0), (0, Lp - L), (0, 0), (0, 0)))
        return t.reshape(bsz, dilation, nb, blk, H, hd)

    def with_prev(t):
        prev = jnp.pad(t, ((0, 0), (0, 0), (1, 0), (0, 0), (0, 0), (0, 0)))[:, :, :-1]
        return jnp.concatenate([prev, t], axis=3)

    qb = to_blocks(q).astype(jnp.float32)
    kk = with_prev(to_blocks(k)).astype(jnp.float32)
    vv = with_prev(to_blocks(v)).astype(jnp.float32)
    scores = jnp.einsum('brnqhd,brnkhd->brnhqk', qb, kk) * (1.0 / np.sqrt(hd)).astype(np.float32)
    qi = jnp.arange(blk)[:, None]
    ki = jnp.arange(2 * blk)[None, :]
    dist = blk + qi - ki
    band = (dist >= 0) & (dist <= span)
    has_prev = (jnp.arange(nb) > 0)[:, None, None] | (ki >= blk)[None]
    mask = band[None] & has_prev
    scores = jnp.where(mask[None, None, :, None], scores, -jnp.inf)
    lse = jax.nn.logsumexp(scores, axis=-1)
    p = jnp.exp(scores - lse[..., None])
    o = jnp.einsum('brnhqk,brnkhd->brnqhd', p, vv)
    o = o.reshape(bsz, dilation, Lp, H, hd)[:, :, :L].transpose(0, 2, 1, 3, 4).reshape(bsz, S, H, hd)
    lse = lse.transpose(0, 1, 2, 4, 3).reshape(bsz, dilation, Lp, H)[:, :, :L]
    lse = lse.transpose(0, 2, 1, 3).reshape(bsz, S, H)
    return o, lse


def mixer_a(h, w_in, v_gain, w_s, b_s, w_out):
    bsz, S, _ = h.shape
    proj = h @ w_in
    u = proj[..., :A_WIDTH]
    v = rms_norm(proj[..., A_WIDTH:2 * A_WIDTH], v_gain)
    z = proj[..., 2 * A_WIDTH:]
    nc = S // CHUNK
    v = v.reshape(bsz, nc, CHUNK, A_GROUPS, A_GROUP_DIM)
    causal = jnp.tril(jnp.ones((CHUNK, CHUNK), dtype=bool))
    ws = jnp.where(causal[None], w_s, jnp.zeros_like(w_s))
    mixed = jnp.einsum('gij,bcjgd->bcigd', ws, v) + b_s.T[None, None, :, :, None]
    mixed = mixed.reshape(bsz, S, A_WIDTH)
    y = u * mixed * jax.nn.silu(z)
    return y @ w_out


def mixer_b(h, w_in, q_gain, k_gain, w_out):
    bsz, S, _ = h.shape
    proj = h @ w_in
    n_qkv = 3 * N_B_GROUPS * B_WIDTH
    qkv = proj[..., :n_qkv].reshape(bsz, S, 3, N_B_GROUPS, B_HEADS, HEAD_DIM)
    z = proj[..., n_qkv:]
    cos, sin = rotary_tables(S)
    outs, lses = [], []
    for g, (window, dilation) in enumerate(B_PATTERNS):
        q = apply_partial_rotary(rms_norm(qkv[:, :, 0, g], q_gain[g]), cos, sin)
        k = apply_partial_rotary(rms_norm(qkv[:, :, 1, g], k_gain[g]), cos, sin)
        o, lse = dilated_window_attention(q, k, qkv[:, :, 2, g], window // dilation, dilation)
        outs.append(o)
        lses.append(lse)
    wgt = jax.nn.softmax(jnp.stack(lses), axis=0)
    o = jnp.einsum('gbsh,gbshd->bshd', wgt, jnp.stack(outs))
    y = o.reshape(bsz, S, B_WIDTH).astype(h.dtype) * jax.nn.silu(z)
    return y @ w_out


def causal_mean(x, window):
    S = x.shape[1]
    c = jnp.cumsum(x.astype(jnp.float32), axis=1)
    c_prev = jnp.pad(c, ((0, 0), (window, 0), (0, 0)))[:, :S]
    cnt = jnp.minimum(jnp.arange(S) + 1, window).astype(jnp.float32)
    return ((c - c_prev) / cnt[None, :, None]).astype(x.dtype)


def mixer_c(h, w_in, w_grp, scale, w_out):
    bsz, S, _ = h.shape
    proj = h @ w_in
    xc = proj[..., :C_WIDTH].reshape(bsz, S, N_POOL, C_GROUP)
    z = proj[..., C_WIDTH:]
    pooled = jnp.stack([causal_mean(xc[:, :, g], w) for g, w in enumerate(POOL_SIZES)], axis=2)
    mixed = jnp.einsum('bsgc,gcd->bsgd', pooled - xc, w_grp).reshape(bsz, S, C_WIDTH) * scale
    y = mixed * jax.nn.silu(z)
    return y @ w_out


def setup_inputs(seed: int = 0) -> dict:
    key = jax.random.key(seed)
    ks = jax.random.split(key, 16)
    f32 = jnp.float32

    def nrm(k, shape, fan_in):
        return jax.random.normal(k, shape, f32) * (fan_in ** -0.5)

    def gain(k, shape):
        return 1.0 + 0.1 * jax.random.normal(k, shape, f32)

    return {
        "x": jax.random.normal(ks[0], (BATCH, SEQ, D_MODEL), f32),
        "norm_gain": gain(ks[1], (DEPTH, D_MODEL)),
        "a_w_in": nrm(ks[2], (N_A, D_MODEL, 3 * A_WIDTH), D_MODEL),
        "a_v_gain": gain(ks[3], (N_A, A_WIDTH)),
        "a_w_s": nrm(ks[4], (N_A, A_GROUPS, CHUNK, CHUNK), CHUNK),
        "a_b_s": gain(ks[5], (N_A, A_GROUPS, CHUNK)),
        "a_w_out": nrm(ks[6], (N_A, A_WIDTH, D_MODEL), A_WIDTH),
        "b_w_in": nrm(ks[7], (N_B, D_MODEL, B_IN_WIDTH), D_MODEL),
        "b_q_gain": gain(ks[8], (N_B, N_B_GROUPS, HEAD_DIM)),
        "b_k_gain": gain(ks[9], (N_B, N_B_GROUPS, HEAD_DIM)),
        "b_w_out": nrm(ks[10], (N_B, B_WIDTH, D_MODEL), B_WIDTH),
        "c_w_in": nrm(ks[11], (N_C, D_MODEL, 2 * C_WIDTH), D_MODEL),
        "c_w_grp": nrm(ks[12], (N_C, N_POOL, C_GROUP, C_GROUP), C_GROUP),
        "c_scale": gain(ks[13], (N_C, C_WIDTH)),
        "c_w_out": nrm(ks[14], (N_C, C_WIDTH, D_MODEL), C_WIDTH),
    }


def reference(x, norm_gain, a_w_in, a_v_gain, a_w_s, a_b_s, a_w_out,
              b_w_in, b_q_gain, b_k_gain, b_w_out,
              c_w_in, c_w_grp, c_scale, c_w_out):
    for i in range(DEPTH):
        kind, j = i % N_MIXERS, i // N_MIXERS
        h = rms_norm(x, norm_gain[i])
        if kind == 0:
            y = mixer_a(h, a_w_in[j], a_v_gain[j], a_w_s[j], a_b_s[j], a_w_out[j])
        elif kind == 1:
            y = mixer_b(h, b_w_in[j], b_q_gain[j], b_k_gain[j], b_w_out[j])
        else:
            y = mixer_c(h, c_w_in[j], c_w_grp[j], c_scale[j], c_w_out[j])
        x = x + y.astype(x.dtype)
    return x
```

```python
import numpy as np
import ml_dtypes
import concourse.bass as bass
import concourse.mybir as mybir
from concourse.bass_utils import run_bass_kernel_spmd

F32 = mybir.dt.float32
BF16 = mybir.dt.bfloat16
AF = mybir.ActivationFunctionType
ALU = mybir.AluOpType
AX = mybir.AxisListType

P = 128
D = 1024
KC = 8
SEQ = 8192
NCORE = 8
OWN = 2048
HALO = 2176
NTOK = OWN + HALO
NCH = NTOK // P
CH0 = 16
NKEEP = NCH - CH0
EPS = 1e-6
NSLOT = 8


class Sync:
    def __init__(self, nc, stack):
        self.nc = nc
        self.h = {"pe": nc.tensor, "act": nc.scalar, "dve": nc.vector,
                  "pool": nc.gpsimd, "sp": nc.sync}
        self.sem = {}
        self.cnt = {}
        self.seen = {}
        for e in self.h:
            self.sem[e] = stack.enter_context(nc.semaphore("s_" + e))
            self.cnt[e] = 0
            self.seen[e] = {}
        self.dsem = {}
        self.dcnt = {}
        self.dnext = {}
        for q in ("sp", "pool", "act"):
            self.dnext[q] = 0
            for s in range(NSLOT):
                k = ("d", q, s)
                self.sem[k] = stack.enter_context(nc.semaphore("d_%s%d" % (q, s)))
                self.dcnt[k] = 0
        self.last_w = {}
        self.readers = {}
        self.alias = {}
        self.n_wait = 0
        self.n_ins = 0

    def _exp(self, keys):
        out = []
        for k in keys:
            if k in self.alias:
                out.extend(self.alias[k])
            else:
                out.append(k)
        return out

    def _wait(self, eng, sig):
        k, v = sig
        if v <= 0:
            return
        if self.seen[eng].get(k, 0) >= v:
            return
        self.h[eng].wait_ge(self.sem[k], v)
        self.seen[eng][k] = v
        self.n_wait += 1

    def _deps(self, eng, reads, writes):
        for r in reads:
            w = self.last_w.get(r)
            if w is not None and not (w[0] == eng and eng == "pe"):
                self._wait(eng, w)
            if r == "pT" or (isinstance(r, tuple) and r[0] == "ps"):
                for rd in self.readers.get(r, ()):
                    if rd[0] != eng:
                        self._wait(eng, rd)
        for wk in writes:
            w = self.last_w.get(wk)
            if w is not None and not (w[0] == eng and eng == "pe"):
                self._wait(eng, w)
            for rd in self.readers.get(wk, ()):
                if not (rd[0] == eng and eng == "pe"):
                    self._wait(eng, rd)

    def _record(self, sig, reads, writes):
        for r in reads:
            self.readers.setdefault(r, []).append(sig)
        for wk in writes:
            self.last_w[wk] = sig
            self.readers[wk] = []

    def op(self, eng, fn, reads=(), writes=(), signal=True):
        reads, writes = self._exp(reads), self._exp(writes)
        self._deps(eng, reads, writes)
        ins = fn(self.h[eng])
        self.n_ins += 1
        if signal:
            ins.then_inc(self.sem[eng], 1)
            self.cnt[eng] += 1
            sig = (eng, self.cnt[eng])
        else:
            sig = (eng, self.cnt[eng] + 1)
        self._record(sig, reads, writes)
        return ins

    def dma(self, q, out, in_, reads=(), writes=()):
        reads, writes = self._exp(reads), self._exp(writes)
        self._deps(q, reads, writes)
        s = self.dnext[q] % NSLOT
        self.dnext[q] += 1
        k = ("d", q, s)
        self._wait(q, (k, self.dcnt[k]))
        ins = self.h[q].dma_start(out=out, in_=in_)
        ins.then_inc(self.sem[k], 16)
        self.dcnt[k] += 16
        self.n_ins += 1
        sig = (k, self.dcnt[k])
        self._record(sig, reads, writes)
        return sig

    def wait_all(self, eng, keys):
        for k in keys:
            w = self.last_w.get(k)
            if w is not None:
                self._wait(eng, w)

    def barrier(self):
        sigs = [(e, self.cnt[e]) for e in self.h] + [(k, v) for k, v in self.dcnt.items()]
        for e in self.h:
            for s in sigs:
                if s[0] != e:
                    self._wait(e, s)
        self.last_w = {}
        self.readers = {}


GROUPS = ((1, 1920), (4, 1536), (16, 0))


def b_blocks():
    out = {}
    i = 0
    for g, (dil, base) in enumerate(GROUPS):
        m_tot = (NTOK - base) // dil
        nblk = -(-m_tot // P)
        for r in range(dil):
            for b in range(nblk):
                nb = min(P, m_tot - P * b)
                out[(g, r, b)] = (i, base + r + dil * P * b, nb, dil)
                i += 1
    return out, i


BLK, NBLK = b_blocks()


class Builder:
    def __init__(self, n_layers=4):
        self.n_layers = n_layers
        self.nc = bass.Bass("TRN2", target_bir_lowering=False)
        nc = self.nc
        di = lambda name, shape: nc.dram_tensor(name, shape, F32, kind="ExternalInput").ap()
        self.xe = di("xe", [NCH, P, D])
        self.ngain = di("ngain", [4, D])
        self.a_w_in = di("a_w_in", [2, D, 6144])
        self.a_vg = di("a_vg", [2, P, 16])
        self.a_wsT = di("a_wsT", [2, P, 8, P])
        self.a_bs = di("a_bs", [2, 1, 1024])
        self.a_w_out = di("a_w_out", [2, 2048, D])
        self.b_w_in = di("b_w_in", [D, 10240])
        self.b_qkg = di("b_qkg", [1, 768])
        self.b_w_out = di("b_w_out", [D, D])
        self.c_w_in = di("c_w_in", [D, 4096])
        self.c_w_grp = di("c_w_grp", [4, 512, 512])
        self.c_sc = di("c_sc", [P, 16])
        self.c_w_out = di("c_w_out", [2048, D])
        self.t_cos = di("t_cos", [P, NBLK, 16])
        self.t_sin = di("t_sin", [P, NBLK, 16])
        self.t_kv = di("t_kv", [P, NBLK])
        self.t_ones = di("t_ones", [P, 6, P])
        self.t_mask = di("t_mask", [P, 2, P])
        self.t_ident = di("t_ident", [P, P])
        self.t_icnt = di("t_icnt", [1, 64])
        self.out = nc.dram_tensor("out", [OWN, D], F32, kind="ExternalOutput").ap()
        self.wbf = {
            "c_w_in": nc.dram_tensor("wbf_c_in", [D, 4096], BF16, kind="Internal").ap(),
            "c_w_out": nc.dram_tensor("wbf_c_out", [2048, D], BF16, kind="Internal").ap(),
            "a_w_in1": nc.dram_tensor("wbf_a_in1", [D, 6144], BF16, kind="Internal").ap(),
            "a_w_out1": nc.dram_tensor("wbf_a_out1", [2048, D], BF16, kind="Internal").ap(),
        }
        self.h1s = nc.dram_tensor("h1s", [P, KC, NTOK], BF16, kind="Internal").ap()
        self.xpark = nc.dram_tensor("xpark", [NKEEP, P, D], F32, kind="Internal").ap()

    def sb(self, st, name, shape, dt):
        self.uid = getattr(self, "uid", 0) + 1
        return st.enter_context(self.nc.sbuf_tensor("sb%d_%s" % (self.uid, name), shape, dt))

    def setup_common(self, st):
        nc, S = self.nc, self.S
        self.ident = self.sb(st, "ident", [P, P], BF16)
        self.ones = self.sb(st, "ones", [P, P], BF16)
        self.nhalf = self.sb(st, "nhalf", [P, 8], F32)
        self.small = self.sb(st, "small", [P, 256], F32)
        self.small_i = 0
        self.junks = [self.sb(st, "junk%d" % i, [P, D], BF16) for i in range(3)]
        self.junk_i = 0
        self.ring = [self.sb(st, "ring%d" % i, [P, 4096], BF16) for i in range(self.R)]
        for i in range(self.R):
            S.alias[("w", i)] = [("w", i, t) for t in range(3)]
        self.mask = self.sb(st, "mask", [P, 2, P], F32)
        S.dma("sp", self.mask[:], self.t_mask, writes=["mask"])
        S.dma("pool", self.ident[:], self.t_ident, writes=["ident"])
        S.op("dve", lambda e: e.memset(self.ones[:], 1.0), writes=["ones"])
        S.op("dve", lambda e: e.memset(self.nhalf[:], -0.5), writes=["nhalf"])
        S.op("dve", lambda e: e.memset(self.small[:], 1.0), writes=[("sm", j) for j in range(256)])

    def junk(self, n):
        i = self.junk_i % len(self.junks)
        self.junk_i += 1
        return self.junks[i][:, 0:n], ("junk", i)

    def smallcol(self, n=1):
        if self.small_i + n > 256:
            self.small_i = 0
        i = self.small_i
        self.small_i += n
        return self.small[:, i:i + n], [("sm", j) for j in range(i, i + n)]

    def wplan(self, units):
        self.units = units
        self.w_issued = 0
        self.w_used = 0

    def wnext(self, tag):
        S = self.S
        i = self.w_used
        assert self.units[i][0] == tag, (i, self.units[i][0], tag)
        while self.w_issued < min(i + self.R - 1, len(self.units)):
            j = self.w_issued
            _, src, shape = self.units[j]
            n = int(np.prod(shape[1:]))
            dst = self.ring[j % self.R][:, 0:n]
            if len(shape) == 3:
                dst = dst.rearrange("p (a b) -> p a b", a=shape[1])
            elif len(shape) == 4:
                dst = dst.rearrange("p (a b c) -> p a b c", a=shape[1], b=shape[2])
            if len(shape) == 4:
                for t in range(shape[2]):
                    S.dma("pool", dst[:, :, t, :], src[:, :, t, :], writes=[("w", j % self.R, t)])
            else:
                q = "sp" if src.dtype == BF16 else "pool"
                S.dma(q, dst, src, writes=[("w", j % self.R)])
            self.w_issued += 1
        self.w_used += 1
        _, src, shape = self.units[i]
        n = int(np.prod(shape[1:]))
        t = self.ring[i % self.R][:, 0:n]
        if len(shape) == 3:
            t = t.rearrange("p (a b) -> p a b", a=shape[1])
        elif len(shape) == 4:
            t = t.rearrange("p (a b c) -> p a b c", a=shape[1], b=shape[2])
        return t, ("w", i % self.R)

    @staticmethod
    def colblock(w, c0, n=512):
        return w.rearrange("(k p) n -> p k n", p=P)[:, :, c0:c0 + n], [P, KC, n]

    def rstd_of(self, ss_ap, ss_keys, inv_n):
        S = self.S
        n = ss_ap.shape[1]
        ms, mk = self.smallcol(n)
        rs, rk = self.smallcol(n)
        S.op("dve", lambda e: e.tensor_scalar(out=ms, in0=ss_ap, scalar1=inv_n, scalar2=EPS, op0=ALU.mult, op1=ALU.add),
             reads=ss_keys, writes=mk)
        S.op("pool", lambda e: e.tensor_tensor(out=rs, in0=ms, in1=self.nhalf[:, 0:n], op=ALU.pow),
             reads=mk + ["nhalf"], writes=rk)
        return rs, rk

    def psnext(self):
        i = self.ps_i % len(self.psb)
        self.ps_i += 1
        return self.psb[i], ("ps", i)

    def hn_prep(self, x_ap, xkeys, gbc, hn, hk):
        S = self.S
        ss, sk = self.smallcol(1)
        jt, jk = self.junk(D)
        S.op("act", lambda e: e.activation(out=jt, in_=x_ap, func=AF.Square, accum_out=ss),
             reads=xkeys, writes=[jk] + sk)
        rs, rk = self.rstd_of(ss, sk, 1.0 / D)
        S.op("dve", lambda e: e.scalar_tensor_tensor(out=hn[:], in0=x_ap, scalar=rs[:, 0:1], in1=gbc, op0=ALU.mult, op1=ALU.mult),
             reads=xkeys + rk + ["gbc"], writes=[hk])

    def hn_trans(self, hn, hk, dst, dkeys):
        S = self.S
        for k in range(KC):
            S.op("pe", lambda e: e.transpose(self.pT[:, k * P:(k + 1) * P], hn[:, k * P:(k + 1) * P], self.ident[:]),
                 reads=[hk, "ident"], writes=["pT"], signal=(k == KC - 1))
        S.op("act", lambda e: e.activation(out=dst, in_=self.pT[:].rearrange("p (k t) -> p k t", k=KC), func=AF.Copy),
             reads=["pT"], writes=dkeys)

    def make_hT(self, x_ap, xkeys, gbc, dst, dkeys):
        hn = self.hn[self.hn_i % len(self.hn)]
        hk = ("hn", self.hn_i % len(self.hn))
        self.hn_i += 1
        self.hn_prep(x_ap, xkeys, gbc, hn, hk)
        self.hn_trans(hn, hk, dst, dkeys)

    def out_proj(self, yT, ykey_fn, nf, chunks_x, wtag, nparts=4):
        S = self.S
        wd = D // nparts
        for part in range(nparts):
            wo, wk = self.wnext(wtag)
            for ci, (x_ap, xkeys) in enumerate(chunks_x):
                bank, bk = self.psnext()
                for f in range(nf):
                    S.op("pe", lambda e: e.matmul(bank[:, 0:wd], lhsT=yT[:, f, ci * P:(ci + 1) * P], rhs=wo[:, f, :],
                                                  start=(f == 0), stop=(f == nf - 1)),
                         reads=[ykey_fn(f), wk], writes=[bk], signal=(f == nf - 1))
                xs = x_ap[:, part * wd:(part + 1) * wd]
                S.op("dve", lambda e: e.tensor_tensor(out=xs, in0=bank[:, 0:wd], in1=xs, op=ALU.add),
                     reads=[bk] + xkeys, writes=xkeys)

    def a_units(self, j, nblocks):
        if j == 1 and self.n_layers >= 4:
            w_in = self.wbf["a_w_in1"]
            wo = self.wbf["a_w_out1"].rearrange("(f p) n -> p f n", p=P)
        else:
            w_in = self.a_w_in[j]
            wo = self.a_w_out[j].rearrange("(f p) n -> p f n", p=P)
        u = []
        for _ in range(nblocks):
            for jv in range(4):
                u.append(("a_v",) + self.colblock(w_in, 2048 + jv * 512))
            for q in range(4):
                u.append(("a_u",) + self.colblock(w_in, q * 512))
                u.append(("a_z",) + self.colblock(w_in, 4096 + q * 512))
            for qq in range(4):
                u.append(("a_o", wo[:, :, qq * 256:(qq + 1) * 256], [P, 16, 256]))
        return u

    def layer_A(self, st, li, j, blocks, get_x, put_x):
        S = self.S
        sb = lambda name, shape, dt: self.sb(st, name, shape, dt)
        gbc = sb("a_gbc", [P, D], F32)
        S.dma("sp", gbc[:], self.ngain[li:li + 1, :].partition_broadcast(P), writes=["gbc"])
        bsb = sb("a_bsb", [P, 8, P], F32)
        S.dma("sp", bsb[:].rearrange("p g i -> p (g i)"), self.a_bs[j].partition_broadcast(P), writes=["bsb"])
        wsT = sb("a_wsT", [P, 8, P], F32)
        S.dma("sp", wsT[:], self.a_wsT[j], writes=["wsT"])
        S.op("dve", lambda e: e.tensor_tensor(out=wsT[:], in0=wsT[:], in1=self.mask[:, 1, :].unsqueeze(1).to_broadcast([P, 8, P]),
                                              op=ALU.mult), reads=["wsT", "mask"], writes=["wsT"])
        vg = sb("a_vg", [P, 16], F32)
        S.dma("sp", vg[:], self.a_vg[j], writes=["vg"])
        self.deferred = []
        hTs = [sb("a_hT%d" % i, [P, KC, 512], BF16) for i in range(2)]
        vraw = [sb("a_vraw%d" % i, [P, 2048], BF16) for i in range(4)]
        wss = [sb("a_wss%d" % i, [P, 8, P], BF16) for i in range(4)]
        yT = sb("a_yT", [P, 16, 512], BF16)
        szt = [sb("a_sz%d" % i, [P, 512], F32) for i in range(2)]
        t1t = [sb("a_t1%d" % i, [P, 512], F32) for i in range(2)]
        hnb = [sb("a_hnb%d" % i, [P, D], BF16) for i in range(4)]

        def prep(bi):
            xs_ = []
            for ci, c in enumerate(blocks[bi]):
                x_ap, xkeys = get_x(c)
                xs_.append((x_ap, xkeys))
                self.hn_prep(x_ap, xkeys, gbc[:], hnb[ci], ("hnb", ci))
            return xs_

        def trans(bi):
            for ci in range(len(blocks[bi])):
                self.hn_trans(hnb[ci], ("hnb", ci), hTs[bi % 2][:, :, ci * P:(ci + 1) * P], [("hT", bi % 2, ci)])

        xs_next = prep(0)
        trans(0)
        for bi, cl in enumerate(blocks):
            n = len(cl)
            N = n * P
            hT = hTs[bi % 2]
            xs = xs_next
            if bi + 1 < len(blocks):
                xs_next = prep(bi + 1)
            hkeys = [("hT", bi % 2, ci) for ci in range(n)]
            ssa, ssk = self.smallcol(16)
            for jv in range(4):
                wt, wk = self.wnext("a_v")
                for ci in range(n):
                    bank, bk = self.psnext()
                    for k in range(KC):
                        S.op("pe", lambda e: e.matmul(bank[:, 0:512], lhsT=hT[:, k, ci * P:(ci + 1) * P], rhs=wt[:, k, :],
                                                      start=(k == 0), stop=(k == KC - 1)),
                             reads=[hkeys[ci], wk], writes=[bk], signal=(k == KC - 1))
                    col = ci * 4 + jv
                    jt, jk = self.junk(512)
                    S.op("act", lambda e: e.activation(out=jt, in_=bank[:, 0:512], func=AF.Square,
                                                       accum_out=ssa[:, col:col + 1]),
                         reads=[bk], writes=[jk, ssk[col]])
                    S.op("dve", lambda e: e.tensor_copy(out=vraw[ci][:, jv * 512:(jv + 1) * 512], in_=bank[:, 0:512]),
                         reads=[bk], writes=[("vraw", ci, jv)])
            st4, st4k = self.smallcol(4)
            S.op("dve", lambda e: e.tensor_reduce(out=st4[:, 0:n], in_=ssa[:, 0:4 * n].rearrange("p (c j) -> p c j", j=4), axis=AX.X, op=ALU.add),
                 reads=ssk[0:4 * n], writes=st4k[0:n])
            rs4, rs4k = self.rstd_of(st4[:, 0:n], st4k[0:n], 1.0 / 2048)
            for ci in range(n):
                S.op("dve", lambda e: e.tensor_scalar(out=wss[ci][:], in0=wsT[:], scalar1=rs4[:, ci:ci + 1], scalar2=None, op0=ALU.mult),
                     reads=["wsT"] + rs4k, writes=[("wss", ci)])
            for fn in self.deferred:
                fn()
            self.deferred = []
            for q in range(4):
                wu, wuk = self.wnext("a_u")
                wz, wzk = self.wnext("a_z")
                for fi in range(4):
                    f = 4 * q + fi
                    g = f // 2
                    pu, puk = self.psnext()
                    pz, pzk = self.psnext()
                    pm, pmk = self.psnext()
                    for k in range(KC):
                        S.op("pe", lambda e: e.matmul(pu[:, 0:N], lhsT=wu[:, k, fi * P:(fi + 1) * P], rhs=hT[:, k, 0:N],
                                                      start=(k == 0), stop=(k == KC - 1)),
                             reads=hkeys + [wuk], writes=[puk], signal=(k == KC - 1))
                    for k in range(KC):
                        S.op("pe", lambda e: e.matmul(pz[:, 0:N], lhsT=wz[:, k, fi * P:(fi + 1) * P], rhs=hT[:, k, 0:N],
                                                      start=(k == 0), stop=(k == KC - 1)),
                             reads=hkeys + [wzk], writes=[pzk], signal=(k == KC - 1))
                    for ci in range(n):
                        S.op("pe", lambda e: e.matmul(pm[:, ci * P:(ci + 1) * P], lhsT=vraw[ci][:, f * P:(f + 1) * P],
                                                      rhs=wss[ci][:, g, :], start=True, stop=True),
                             reads=[("vraw", ci, f // 4), ("wss", ci)], writes=[pmk], signal=(ci == n - 1))
                    sz = szt[f % 2]
                    t1 = t1t[f % 2]
                    S.op("act", lambda e: e.activation(out=sz[:, 0:N], in_=pz[:, 0:N], func=AF.Silu),
                         reads=[pzk], writes=[("sz", f % 2)])
                    S.op("dve", lambda e: e.scalar_tensor_tensor(
                        out=t1[:, 0:N].rearrange("p (c i) -> p c i", c=n), in0=pm[:, 0:N].rearrange("p (c i) -> p c i", c=n),
                        scalar=vg[:, f:f + 1], in1=bsb[:, g, :].unsqueeze(1).to_broadcast([P, n, P]),
                        op0=ALU.mult, op1=ALU.add),
                        reads=[pmk, "vg", "bsb"], writes=[("t1", f % 2)])
                    S.op("dve", lambda e: e.tensor_tensor(out=t1[:, 0:N], in0=pu[:, 0:N], in1=t1[:, 0:N], op=ALU.mult),
                         reads=[puk, ("t1", f % 2)], writes=[("t1", f % 2)])
                    S.op("dve", lambda e: e.tensor_tensor(out=yT[:, f, 0:N], in0=t1[:, 0:N], in1=sz[:, 0:N], op=ALU.mult),
                         reads=[("t1", f % 2), ("sz", f % 2)], writes=[("yT", f)])
            if bi + 1 < len(blocks):
                trans(bi + 1)
            self.out_proj(yT, lambda f: ("yT", f), 16, xs, "a_o")
            for ci, c in enumerate(cl):
                put_x(c, xs[ci][0], xs[ci][1])
        for fn in self.deferred:
            fn()
        self.deferred = []

    def b_units(self):
        w = self.b_w_in.rearrange("(k p) (j n) -> p k j n", p=P, n=P)
        u = []
        for h in range(8):
            for g in range(3):
                j0 = g * 8 + h
                u.append(("b_qkv", w[:, :, j0:j0 + 49:24, :], [P, KC, 3, P]))
            u.append(("b_z", w[:, :, 72 + h, :], [P, KC, P]))
        wo = self.b_w_out.rearrange("(f p) n -> p f n", p=P)
        for half in range(2):
            u.append(("b_o", wo[:, :, half * 512:(half + 1) * 512], [P, 8, 512]))
        return u

    def layer_B(self, st):
        S = self.S
        sb = lambda name, shape, dt: self.sb(st, name, shape, dt)
        NQ = NTOK - 2048
        hTB = sb("b_hT", [P, KC, NTOK], BF16)
        for k in range(KC):
            S.dma("sp", hTB[:, k, :], self.h1s[:, k, :], writes=["hTB"])
        cos = sb("b_cos", [P, NBLK, 16], F32)
        sin = sb("b_sin", [P, NBLK, 16], F32)
        kv = sb("b_kv", [P, NBLK], F32)
        S.dma("sp", cos[:], self.t_cos, writes=["cos"])
        S.dma("sp", sin[:], self.t_sin, writes=["sin"])
        S.dma("sp", kv[:], self.t_kv, writes=["kv"])
        onesv = sb("b_onesv", [P, 6, P], BF16)
        S.dma("pool", onesv[:], self.t_ones, writes=["onesv"])
        qkg = sb("b_qkg", [P, 3, 2, P], F32)
        S.dma("sp", qkg[:].rearrange("p g j d -> p (g j d)"), self.b_qkg.partition_broadcast(P), writes=["qkg"])
        S.op("dve", lambda e: e.tensor_scalar(out=qkg[:], in0=qkg[:], scalar1=float(np.sqrt(128.0)), scalar2=None, op0=ALU.mult),
             reads=["qkg"], writes=["qkg"])
        eps128 = sb("b_eps", [P, 8], F32)
        S.op("dve", lambda e: e.memset(eps128[:], 128.0 * EPS), writes=["eps128"])
        maskb = sb("b_maskb", [P, 2, P], BF16)
        S.op("dve", lambda e: e.tensor_copy(out=maskb[:], in_=self.mask[:]), reads=["mask"], writes=["maskb"])
        OD = sb("b_OD", [P, 2, NQ], F32)
        yTB = sb("b_yT", [P, 8, NQ], BF16)
        NQK, NRT, NV, NT, NE = 8, 3, 12, 5, 5
        qkall = sb("b_qkall", [P, NQK, 2, P], BF16)
        S.op("dve", lambda e: e.memset(qkall[:], 0.0), writes=[(("qk", i), j) for i in range(NQK) for j in range(2)])
        rtmp = [sb("b_rt%d" % i, [P, 4, 2, 2, 16], F32) for i in range(NRT)]
        pairst = {}
        Vt = [sb("b_V%d" % i, [P, P], BF16) for i in range(NV)]
        qkT = [sb("b_qkT%d" % i, [P, 2, P], BF16) for i in range(NT)]
        Et = [sb("b_E%d" % i, [P, 2, P], BF16) for i in range(NE)]
        PTt = [sb("b_PT%d" % i, [P, 2, P], BF16) for i in range(NE)]
        sqj = [sb("b_sq%d" % i, [P, P], BF16) for i in range(4)]
        szt = [sb("b_sz%d" % i, [P, 512], F32) for i in range(2)]
        xp = [sb("b_xp%d" % i, [P, D], F32) for i in range(2)]
        scale = 1.0 / float(np.sqrt(128.0))
        cnt = {"s": 0, "o": 0, "sq": 0}
        wcur = {}

        def stage_P(bs, n):
            nb, start, dil = bs["nb"], bs["start"], bs["dil"]
            bank, bk = self.psb[n % 4], ("ps", n % 4)
            bs["bank"], bs["bk"] = bank, bk
            if bs["newg"]:
                wcur["w"] = self.wnext("b_qkv")
            wq, wqk = wcur["w"]
            stop_tok = start + dil * (nb - 1) + 1
            j0 = bs["j0"]
            for k in range(KC):
                S.op("pe", lambda e: e.matmul(bank[:nb, j0 * P:384], lhsT=hTB[:, k, start:stop_tok:dil],
                                              rhs=wq[:, k, j0:3, :].rearrange("p a b -> p (a b)"),
                                              start=(k == 0), stop=(k == KC - 1)),
                     reads=["hTB", wqk], writes=[bk], signal=(k == KC - 1))

        def stage_E(bs, n):
            nb, idx, g = bs["nb"], bs["idx"], bs["g"]
            bank, bk = bs["bank"], bs["bk"]
            if n % 2 == 0:
                pairst["ss"] = self.smallcol(4)
            ssa, ska = pairst["ss"]
            o = 2 * (n % 2)
            ss, sk = ssa[:, o:o + 2], ska[o:o + 2]
            for j in range(bs["j0"], 2):
                jt = sqj[cnt["sq"] % 4]
                jk = ("sqj", cnt["sq"] % 4)
                cnt["sq"] += 1
                S.op("act", lambda e: e.activation(out=jt[:nb, :], in_=bank[:nb, j * P:(j + 1) * P], func=AF.Square,
                                                   accum_out=ss[:nb, j:j + 1]),
                     reads=[bk], writes=[jk, sk[j]])
            V = Vt[n % NV]
            Vk = ("V", n % NV)
            S.op("act", lambda e: e.activation(out=V[:nb, :], in_=bank[:nb, 2 * P:3 * P], func=AF.Copy, scale=kv[:nb, idx:idx + 1]),
                 reads=[bk, "kv"], writes=[Vk])
            bs.update(V=V, Vk=Vk, ssa=ssa, ska=ska, o=o)

        def stage_E2(blks, n0, pi):
            ssa, ska = blks[0]["ssa"], blks[0]["ska"]
            ms, mk = self.smallcol(4)
            rs, rk = self.smallcol(4)
            S.op("pool", lambda e: e.tensor_tensor(out=ms, in0=ssa, in1=eps128[:, 0:4], op=ALU.add), reads=ska + ["eps128"], writes=mk)
            S.op("pool", lambda e: e.tensor_tensor(out=rs, in0=ms, in1=self.nhalf[:, 0:4], op=ALU.pow), reads=mk + ["nhalf"], writes=rk)
            for i, bs in enumerate(blks):
                n = n0 + i
                nb, g, bank, bk, o = bs["nb"], bs["g"], bs["bank"], bs["bk"], bs["o"]
                qk = qkall[:, n % NQK, :, :]
                qkk = ("qk", n % NQK)
                for j in range(bs["j0"], 2):
                    S.op("dve", lambda e: e.scalar_tensor_tensor(out=qk[:nb, j, :], in0=bank[:nb, j * P:(j + 1) * P],
                                                                  scalar=rs[:nb, o + j:o + j + 1], in1=qkg[:nb, g, j, :],
                                                                  op0=ALU.mult, op1=ALU.mult),
                         reads=[bk] + rk + ["qkg"], writes=[(qkk, j)])
                bs.update(qk=qk, qkk=qkk)
            reng = "pool"
            rt = rtmp[pi % NRT]
            rtk = ("rt", pi % NRT)
            if len(blks) == 2 and blks[0]["nb"] == P and blks[1]["nb"] == P:
                groups = [(blks, qkall[:, n0 % NQK:n0 % NQK + 2, :, :], P, 2)]
            else:
                groups = [([bs], qkall[:, (n0 + i) % NQK:(n0 + i) % NQK + 1, :, :], bs["nb"], 1) for i, bs in enumerate(blks)]
            for bl, qv, nb, m in groups:
                idx = bl[0]["idx"]
                keys = [(b_["qkk"], j) for b_ in bl for j in range(2)]
                x1 = qv[:nb, :, :, 0:16]
                x2 = qv[:nb, :, :, 16:32]
                cb = cos[:nb, idx:idx + m, :].unsqueeze(2).to_broadcast([nb, m, 2, 16])
                sbb = sin[:nb, idx:idx + m, :].unsqueeze(2).to_broadcast([nb, m, 2, 16])
                tv = lambda t: rt[:nb, t, 0:m, :, :]
                for t, (a_, b_) in enumerate(((x1, cb), (x2, sbb), (x2, cb), (x1, sbb))):
                    S.op(reng, lambda e: e.tensor_tensor(out=tv(t), in0=a_, in1=b_, op=ALU.mult),
                         reads=keys + ["cos", "sin"], writes=[(rtk, t)])
                S.op(reng, lambda e: e.tensor_tensor(out=x1, in0=tv(0), in1=tv(1), op=ALU.subtract),
                     reads=[(rtk, 0), (rtk, 1)], writes=keys)
                S.op(reng, lambda e: e.tensor_tensor(out=x2, in0=tv(2), in1=tv(3), op=ALU.add),
                     reads=[(rtk, 2), (rtk, 3)], writes=keys)

        def stage_T(bs, n):
            nb, qk, qkk = bs["nb"], bs["qk"], bs["qkk"]
            j0 = bs["j0"]
            for j in range(j0, 2):
                S.op("pe", lambda e: e.transpose(self.pT[:, j * P:j * P + nb], qk[:nb, j, :], self.ident[:nb, :nb]),
                     reads=[(qkk, 0), (qkk, 1), "ident"], writes=["pT"], signal=(j == 1))
            T = qkT[n % NT]
            Tk = ("qkT", n % NT)
            S.op("act", lambda e: e.activation(out=T[:, j0:2, 0:nb], in_=self.pT[:, 0:2 * P].rearrange("p (j t) -> p j t", j=2)[:, j0:2, 0:nb],
                                               func=AF.Copy),
                 reads=["pT"], writes=[Tk])
            bs.update(T=T, Tk=Tk)

        def stage_S(cur, prev):
            nq = cur["nb"]
            si = cnt["s"]
            cnt["s"] += 1
            sbank, sbk = self.psb[4], ("ps", 4)
            for t, kb in enumerate((prev, cur)):
                nk = kb["nb"]
                S.op("pe", lambda e: e.matmul(sbank[:nk, t * P:t * P + nq], lhsT=kb["T"][:, 1, 0:nk], rhs=cur["T"][:, 0, 0:nq],
                                              start=True, stop=True),
                     reads=[kb["Tk"], cur["Tk"]], writes=[sbk], signal=(t == 1))
            E = Et[si % NE]
            PT = PTt[si % NE]
            Ek, PTk = ("E", si % NE), ("PT", si % NE)
            if nq == P:
                S.op("act", lambda e: e.activation(out=E[:].rearrange("p a b -> p (a b)"), in_=sbank[:, 0:2 * P], func=AF.Exp, scale=scale),
                     reads=[sbk], writes=[Ek])
                S.op("dve", lambda e: e.tensor_tensor(out=PT[:], in0=E[:], in1=maskb[:], op=ALU.mult),
                     reads=[Ek, "maskb"], writes=[PTk])
            else:
                for t, kb in enumerate((prev, cur)):
                    nk = kb["nb"]
                    S.op("act", lambda e: e.activation(out=E[:nk, t, 0:nq], in_=sbank[:nk, t * P:t * P + nq], func=AF.Exp, scale=scale),
                         reads=[sbk], writes=[Ek])
                    S.op("dve", lambda e: e.tensor_tensor(out=PT[:nk, t, 0:nq], in0=E[:nk, t, 0:nq], in1=self.mask[:nk, t, 0:nq], op=ALU.mult),
                         reads=[Ek, "mask"], writes=[PTk])
            cur.update(PT=PT, PTk=PTk)

        def stage_O(cur, prev, first):
            nq = cur["nb"]
            oi = cnt["o"]
            cnt["o"] += 1
            obank, obk = self.psb[5 + oi % 2], ("ps", 5 + oi % 2)
            PT, PTk = cur["PT"], cur["PTk"]
            for t, kb in enumerate((prev, cur)):
                nk = kb["nb"]
                S.op("pe", lambda e: e.matmul(obank[:, 0:nq], lhsT=kb["V"][:nk, :], rhs=PT[:nk, t, 0:nq],
                                              start=(t == 0), stop=(t == 1)),
                     reads=[kb["Vk"], PTk], writes=[obk], signal=False)
            for t, kb in enumerate((prev, cur)):
                nk = kb["nb"]
                ov = self.ones[:nk, :] if kb["b"] >= 2 else onesv[:nk, 2 * kb["g"] + kb["b"], :]
                S.op("pe", lambda e: e.matmul(obank[:, P:P + nq], lhsT=ov, rhs=PT[:nk, t, 0:nq],
                                              start=(t == 0), stop=(t == 1)),
                     reads=["ones", "onesv", PTk], writes=[obk], signal=(t == 1))
            q0 = cur["start"] - 2048
            q1 = q0 + cur["dil"] * (nq - 1) + 1
            osl = OD[:, :, q0:q1:cur["dil"]]
            src = obank[:, 0:2 * P].rearrange("p (a b) -> p a b", a=2)[:, :, 0:nq]
            if first:
                S.op("act", lambda e: e.activation(out=osl, in_=src, func=AF.Copy), reads=[obk], writes=["OD"])
            else:
                S.op("dve", lambda e: e.tensor_tensor(out=osl, in0=src, in1=osl, op=ALU.add), reads=[obk, "OD"], writes=["OD"])

        conv = []
        if self.n_layers >= 3:
            conv += [(self.wbf["c_w_in"][i * P:(i + 1) * P, :], self.c_w_in[i * P:(i + 1) * P, :]) for i in range(8)]
            conv += [(self.wbf["c_w_out"][i * P:(i + 1) * P, :], self.c_w_out[i * P:(i + 1) * P, :]) for i in range(16)]
        if self.n_layers >= 4:
            conv += [(self.wbf["a_w_in1"][i * P:(i + 1) * P, :], self.a_w_in[1][i * P:(i + 1) * P, :]) for i in range(8)]
            conv += [(self.wbf["a_w_out1"][i * P:(i + 1) * P, :], self.a_w_out[1][i * P:(i + 1) * P, :]) for i in range(16)]
        for h in range(8):
            n_c = -(-len(conv) // 8)
            for ci_, (dst_, src_) in enumerate(conv[h * n_c:(h + 1) * n_c]):
                S.dma("pool", dst_, src_, writes=[("wconv", h, ci_)])
            blist = []
            for g in range(3):
                dil, base = GROUPS[g]
                nblk = -(-((NTOK - base) // dil) // P)
                for r in range(dil):
                    for b in range(nblk):
                        idx, start, nb, _ = BLK[(g, r, b)]
                        blist.append(dict(g=g, r=r, b=b, idx=idx, start=start, nb=nb, dil=dil, newg=(r == 0 and b == 0),
                                           j0=(1 if b == 0 else 0)))
            NB_ = len(blist)
            npair = 0
            for n in range(NB_ + 7):
                if n < NB_:
                    stage_P(blist[n], n)
                    stage_E(blist[n], n)
                if 0 <= n - 4 < NB_:
                    stage_T(blist[n - 4], n - 4)
                if 0 <= n - 5 < NB_ and blist[n - 5]["b"] >= 1:
                    stage_S(blist[n - 5], blist[n - 6])
                if 0 <= n - 7 < NB_ and blist[n - 7]["b"] >= 1:
                    stage_O(blist[n - 7], blist[n - 8], first=(blist[n - 7]["g"] == 0))
                if n < NB_ and n % 2 == 1:
                    stage_E2(blist[n - 1:n + 1], n - 1, npair)
                    npair += 1
                elif n == NB_ - 1:
                    stage_E2(blist[n:n + 1], n, npair)
                    npair += 1
            wz, wzk = self.wnext("b_z")
            Den = OD[:, 1, :]
            Oacc = OD[:, 0, :]
            S.op("dve", lambda e: e.tensor_scalar(out=Den, in0=Den, scalar1=1e-30, scalar2=None, op0=ALU.max),
                 reads=["OD"], writes=["OD"])
            S.op("dve", lambda e: e.reciprocal(out=Den, in_=Den), reads=["OD"], writes=["OD"])
            S.op("dve", lambda e: e.tensor_tensor(out=Oacc, in0=Oacc, in1=Den, op=ALU.mult),
                 reads=["OD"], writes=["OD"])
            for tb in range(5):
                n = min(512, NQ - tb * 512)
                zb, zbk = self.psnext()
                for k in range(KC):
                    S.op("pe", lambda e: e.matmul(zb[:, 0:n], lhsT=wz[:, k, :], rhs=hTB[:, k, 2048 + tb * 512:2048 + tb * 512 + n],
                                                  start=(k == 0), stop=(k == KC - 1)),
                         reads=["hTB", wzk], writes=[zbk], signal=(k == KC - 1))
                sz = szt[tb % 2]
                S.op("act", lambda e: e.activation(out=sz[:, 0:n], in_=zb[:, 0:n], func=AF.Silu), reads=[zbk], writes=[("sz", tb % 2)])
                S.op("dve", lambda e: e.tensor_tensor(out=yTB[:, h, tb * 512:tb * 512 + n], in0=OD[:, 0, tb * 512:tb * 512 + n],
                                                      in1=sz[:, 0:n], op=ALU.mult),
                     reads=["OD", ("sz", tb % 2)], writes=[("yTB", h)])
        wo0, wo0k = self.wnext("b_o")
        wo1, wo1k = self.wnext("b_o")
        for ci in range(NKEEP):
            x = xp[ci % 2]
            xk = ("xp", ci % 2)
            S.dma("sp", x[:], self.xpark[ci], reads=[("xpark", ci)], writes=[xk])
            for half, (wo, wok) in enumerate(((wo0, wo0k), (wo1, wo1k))):
                bank, bk = self.psnext()
                for hh in range(8):
                    S.op("pe", lambda e: e.matmul(bank[:, 0:512], lhsT=yTB[:, hh, ci * P:(ci + 1) * P], rhs=wo[:, hh, :],
                                                  start=(hh == 0), stop=(hh == 7)),
                         reads=[("yTB", hh), wok], writes=[bk], signal=(hh == 7))
                xs = x[:, half * 512:(half + 1) * 512]
                S.op("dve", lambda e: e.tensor_tensor(out=xs, in0=bank[:, 0:512], in1=xs, op=ALU.add), reads=[bk, xk], writes=[xk])
            S.dma("sp", self.xpark[ci], x[:], reads=[xk], writes=[("xpark", ci)])

    def c_units(self):
        w_in = self.wbf["c_w_in"]
        wo = self.wbf["c_w_out"].rearrange("(f p) n -> p f n", p=P)
        u = []
        for g in range(4):
            u.append(("c_x",) + self.colblock(w_in, g * 512))
        for _ in range(4):
            for g in range(4):
                u.append(("c_x",) + self.colblock(w_in, g * 512))
                u.append(("c_z",) + self.colblock(w_in, 2048 + g * 512))
            for qq in range(4):
                u.append(("c_o", wo[:, :, qq * 256:(qq + 1) * 256], [P, 16, 256]))
        return u

    def layer_C(self, st, X, blocks):
        S = self.S
        sb = lambda name, shape, dt: self.sb(st, name, shape, dt)
        gbc = sb("c_gbc", [P, D], F32)
        S.dma("sp", gbc[:], self.ngain[2:3, :].partition_broadcast(P), writes=["gbc"])
        csc = sb("c_sc", [P, 16], F32)
        S.dma("sp", csc[:], self.c_sc, writes=["csc"])
        icnt = sb("c_icnt", [P, 4, 16], F32)
        S.dma("sp", icnt[:].rearrange("p g i -> p (g i)"), self.t_icnt.partition_broadcast(P), writes=["icnt"])
        wgrp = sb("c_wgrp", [P, 4, 4, 512], BF16)
        for g in range(4):
            S.dma("pool", wgrp[:, g, :, :], self.c_w_grp[g].rearrange("(fi p) n -> p fi n", p=P), writes=[("wgrp", g)])
        self.hn = [sb("c_hn%d" % i, [P, D], BF16) for i in range(2)]
        self.hn_i = 0
        hTs = [sb("c_hT%d" % i, [P, KC, 512], BF16) for i in range(2)]
        yT = sb("c_yT", [P, 16, 512], BF16)
        dT = [sb("c_dT%d" % i, [P, 512], BF16) for i in range(4)]
        xcb = [sb("c_xcb%d" % i, [P, 528], F32) for i in range(2)]
        sab = [sb("c_sab%d" % i, [P, 528], F32) for i in range(2)]
        carry = sb("c_carry", [P, 16, 16], F32)
        fix = sb("c_fix", [P, 16], F32)
        szt = [sb("c_sz%d" % i, [P, 512], F32) for i in range(2)]
        hT16 = hTs[1][:, :, 0:P]
        self.make_hT(X[:, 0, :], [("X", 0)], gbc[:], hT16, [("hT", 1, 0)])
        for g in range(4):
            wx, wxk = self.wnext("c_x")
            for fi in range(4):
                f = 4 * g + fi
                bank, bk = self.psnext()
                for k in range(KC):
                    S.op("pe", lambda e: e.matmul(bank[:, 0:16], lhsT=wx[:, k, fi * P:(fi + 1) * P], rhs=hT16[:, k, 112:128],
                                                  start=(k == 0), stop=(k == KC - 1)),
                         reads=[("hT", 1, 0), wxk], writes=[bk], signal=(k == KC - 1))
                S.op("act", lambda e: e.activation(out=carry[:, f, :], in_=bank[:, 0:16], func=AF.Copy),
                     reads=[bk], writes=[("carry", f)])
        xi = 0
        hnb = [sb("c_hnb%d" % i, [P, D], BF16) for i in range(4)]

        def prep(bi):
            xs_ = []
            for ci, c in enumerate(blocks[bi]):
                x_ap, xkeys = X[:, c - CH0, :], [("X", c - CH0)]
                xs_.append((x_ap, xkeys))
                self.hn_prep(x_ap, xkeys, gbc[:], hnb[ci], ("hnb", ci))
            return xs_

        def trans(bi):
            for ci in range(len(blocks[bi])):
                self.hn_trans(hnb[ci], ("hnb", ci), hTs[bi % 2][:, :, ci * P:(ci + 1) * P], [("hT", bi % 2, ci)])

        xs_next = prep(0)
        trans(0)
        for bi, cl in enumerate(blocks):
            n = len(cl)
            N = n * P
            hT = hTs[bi % 2]
            xs = xs_next
            if bi + 1 < len(blocks):
                xs_next = prep(bi + 1)
            hkeys = [("hT", bi % 2, ci) for ci in range(n)]
            for g in range(4):
                w = 2 << g
                wx, wxk = self.wnext("c_x")
                wz, wzk = self.wnext("c_z")
                for fi in range(4):
                    f = 4 * g + fi
                    px, pxk = self.psnext()
                    for k in range(KC):
                        S.op("pe", lambda e: e.matmul(px[:, 0:N], lhsT=wx[:, k, fi * P:(fi + 1) * P], rhs=hT[:, k, 0:N],
                                                      start=(k == 0), stop=(k == KC - 1)),
                             reads=hkeys + [wxk], writes=[pxk], signal=(k == KC - 1))
                    xb = xcb[xi % 2]
                    xk = ("xcb", xi % 2)
                    xi += 1
                    S.op("act", lambda e: e.activation(out=xb[:, 16:16 + N], in_=px[:, 0:N], func=AF.Copy),
                         reads=[pxk], writes=[xk])
                    S.op("act", lambda e: e.activation(out=xb[:, 0:16], in_=carry[:, f, :], func=AF.Copy),
                         reads=[("carry", f)], writes=[xk])
                    S.op("act", lambda e: e.activation(out=carry[:, f, :], in_=xb[:, N:N + 16], func=AF.Copy),
                         reads=[xk], writes=[("carry", f)])
                    cur, ck = xb, xk
                    step = 1
                    si = 0
                    while step < w:
                        i0 = 2 * step - 1
                        nxt, nk = sab[si % 2], ("sab", si % 2)
                        si += 1
                        S.op("dve", lambda e: e.tensor_tensor(out=nxt[:, i0:16 + N], in0=cur[:, i0:16 + N],
                                                              in1=cur[:, i0 - step:16 + N - step], op=ALU.add),
                             reads=[ck], writes=[nk])
                        cur, ck = nxt, nk
                        step *= 2
                    S.op("dve", lambda e: e.scalar_tensor_tensor(out=dT[fi][:, 0:N], in0=cur[:, 16:16 + N], scalar=1.0 / w,
                                                                  in1=xb[:, 16:16 + N], op0=ALU.mult, op1=ALU.subtract),
                         reads=[ck, xk], writes=[("dT", fi)])
                    if bi == 0:
                        S.op("dve", lambda e: e.tensor_tensor(out=fix[:], in0=cur[:, 16:32], in1=icnt[:, g, :], op=ALU.mult),
                             reads=[ck, "icnt"], writes=["fix"])
                        S.op("dve", lambda e: e.tensor_tensor(out=dT[fi][:, 0:16], in0=fix[:], in1=xb[:, 16:32], op=ALU.subtract),
                             reads=["fix", xk, ("dT", fi)], writes=[("dT", fi)])
                for fo in range(4):
                    f = 4 * g + fo
                    pm, pmk = self.psnext()
                    pz, pzk = self.psnext()
                    for fi in range(4):
                        S.op("pe", lambda e: e.matmul(pm[:, 0:N], lhsT=wgrp[:, g, fi, fo * P:(fo + 1) * P], rhs=dT[fi][:, 0:N],
                                                      start=(fi == 0), stop=(fi == 3)),
                             reads=[("wgrp", g), ("dT", fi)], writes=[pmk], signal=(fi == 3))
                    for k in range(KC):
                        S.op("pe", lambda e: e.matmul(pz[:, 0:N], lhsT=wz[:, k, fo * P:(fo + 1) * P], rhs=hT[:, k, 0:N],
                                                      start=(k == 0), stop=(k == KC - 1)),
                             reads=hkeys + [wzk], writes=[pzk], signal=(k == KC - 1))
                    sz = szt[f % 2]
                    S.op("act", lambda e: e.activation(out=sz[:, 0:N], in_=pz[:, 0:N], func=AF.Silu),
                         reads=[pzk], writes=[("sz", f % 2)])
                    S.op("dve", lambda e: e.scalar_tensor_tensor(out=yT[:, f, 0:N], in0=pm[:, 0:N], scalar=csc[:, f:f + 1],
                                                                  in1=sz[:, 0:N], op0=ALU.mult, op1=ALU.mult),
                         reads=[pmk, "csc", ("sz", f % 2)], writes=[("yT", f)])
            if bi + 1 < len(blocks):
                trans(bi + 1)
            self.out_proj(yT, lambda f: ("yT", f), 16, xs, "c_o")

    def build(self):
        from contextlib import ExitStack
        nc = self.nc
        self.R = 4
        A_BLOCKS0 = [[0, 1, 2, 3], [4, 5, 6, 7], [8, 9, 10, 11], [12, 13, 14, 15], [16],
                     [17, 18, 19, 20], [21, 22, 23, 24], [25, 26, 27, 28], [29, 30, 31, 32]]
        OWN_BLOCKS = A_BLOCKS0[5:]
        units = self.a_units(0, len(A_BLOCKS0))
        if self.n_layers >= 2:
            units += self.b_units()
        if self.n_layers >= 3:
            units += self.c_units()
        if self.n_layers >= 4:
            units += self.a_units(1, len(OWN_BLOCKS))
        self.wplan(units)
        with ExitStack() as st:
            self.S = S = Sync(nc, st)
            self.st0 = st
            self.setup_common(st)
            self.psb = [st.enter_context(nc.psum_tensor("psb%d" % i, [P, 512], F32)) for i in range(7)]
            self.ps_i = 0
            self.pT = st.enter_context(nc.psum_tensor("pT", [P, 1024], BF16))
            last = self.n_layers
            with ExitStack() as l0:
                xin = [self.sb(l0, "xin%d" % i, [P, D], F32) for i in range(8)]
                gbc1 = self.sb(l0, "gbc1", [P, D], F32)
                S.dma("sp", gbc1[:], self.ngain[1:2, :].partition_broadcast(P), writes=["gbc1"])
                h1T = [self.sb(l0, "h1T%d" % i, [P, KC, P], BF16) for i in range(2)]
                state = {"i": 0, "slot": {}, "h": 0}
                h1n = [self.sb(l0, "h1n%d" % i, [P, D], BF16) for i in range(4)]

                def get_x(c):
                    s = state["i"] % 8
                    state["i"] += 1
                    state["slot"][c] = s
                    S.dma("sp", xin[s][:], self.xe[c], writes=[("xin", s)])
                    return xin[s][:], [("xin", s)]

                def put_x(c, ap, keys):
                    if last == 1:
                        if c >= 17:
                            S.dma("sp", self.out[(c - 17) * P:(c - 16) * P, :], ap, reads=keys)
                        return
                    if c >= CH0:
                        S.dma("sp", self.xpark[c - CH0], ap, reads=keys, writes=[("xpark", c - CH0)])
                    i = state["h"] % 4
                    state["h"] += 1
                    hn_, hk_ = h1n[i], ("h1n", i)
                    t = h1T[i % 2]
                    tk = [("h1T", i % 2)]
                    self.hn_prep(ap, keys + ["gbc1"], gbc1[:], hn_, hk_)

                    def later(c=c, hn_=hn_, hk_=hk_, t=t, tk=tk):
                        self.hn_trans(hn_, hk_, t[:], tk)
                        S.dma("sp", self.h1s[:, :, c * P:(c + 1) * P], t[:], reads=tk, writes=[("h1s", c)])
                    self.deferred.append(later)

                self.layer_A(l0, 0, 0, A_BLOCKS0, get_x, put_x)
                S.barrier()
            if last >= 2:
                with ExitStack() as l1:
                    self.layer_B(l1)
                    S.barrier()
            if last >= 3:
                with ExitStack() as l23:
                    X = self.sb(l23, "X", [P, NKEEP, D], F32)
                    for c in range(NKEEP):
                        S.dma("sp", X[:, c, :], self.xpark[c], writes=[("X", c)])
                    with ExitStack() as l2:
                        self.layer_C(l2, X, OWN_BLOCKS)
                        S.barrier()
                    if last >= 4:
                        with ExitStack() as l3:
                            def get_x3(c):
                                return X[:, c - CH0, :], [("X", c - CH0)]

                            def put_x3(c, ap, keys):
                                S.dma("sp", self.out[(c - 17) * P:(c - 16) * P, :], ap, reads=keys)
                            self.layer_A(l3, 3, 1, OWN_BLOCKS, get_x3, put_x3)
                            S.barrier()
                    else:
                        for c in range(17, NCH):
                            S.dma("sp", self.out[(c - 17) * P:(c - 16) * P, :], X[:, c - CH0, :], reads=[("X", c - CH0)])
            elif last == 2:
                with ExitStack() as lx:
                    xt = self.sb(lx, "xdbg", [P, D], F32)
                    for c in range(17, NCH):
                        S.dma("sp", xt[:], self.xpark[c - CH0], writes=["xdbg"])
                        S.dma("sp", self.out[(c - 17) * P:(c - 16) * P, :], xt[:], reads=["xdbg"])
            S.barrier()
            assert self.w_used == len(self.units), (self.w_used, len(self.units))
            print("instructions", S.n_ins, "waits", S.n_wait)
        return nc


def _tables(core):
    b, q = divmod(core, 4)
    t0 = q * OWN - HALO
    half = 16
    inv_freq = np.power(np.float32(500000.0), -np.arange(half, dtype=np.float32) / np.float32(half)).astype(np.float32)
    cos = np.zeros((P, NBLK, 16), np.float32)
    sin = np.zeros((P, NBLK, 16), np.float32)
    kv = np.zeros((P, NBLK), np.float32)
    for (g, r, bb), (i, start, nb, dil) in BLK.items():
        pos = t0 + start + dil * np.arange(nb)
        valid = pos >= 0
        ang = np.maximum(pos, 0).astype(np.float32)[:, None] * inv_freq[None, :]
        c, s = np.cos(ang).astype(np.float32), np.sin(ang).astype(np.float32)
        cos[:nb, i, :] = c
        sin[:nb, i, :] = s
        kv[:nb, i] = valid.astype(np.float32)
    onesv = np.zeros((P, 6, P), np.float32)
    for g in range(3):
        for bb in range(2):
            onesv[:, 2 * g + bb, :] = kv[:, BLK[(g, 0, bb)][0]][:, None]
    kk = np.arange(P)[:, None]
    qq = np.arange(P)[None, :]
    mask = np.stack([(kk >= qq), (kk <= qq)], axis=1).astype(np.float32)
    icnt = np.zeros((1, 64), np.float32)
    for gi, w in enumerate((2, 4, 8, 16)):
        pos = q * OWN + np.arange(16)
        icnt[0, gi * 16:(gi + 1) * 16] = 1.0 / np.minimum(pos + 1, w)
    return cos, sin, kv, mask, icnt, onesv


_NC_CACHE = {}


def kernel(x, norm_gain, a_w_in, a_v_gain, a_w_s, a_b_s, a_w_out, b_w_in, b_q_gain, b_k_gain, b_w_out,
           c_w_in, c_w_grp, c_scale, c_w_out, _n_layers=4):
    f = lambda a: np.ascontiguousarray(np.asarray(a, dtype=np.float32))
    x = f(x)
    if _n_layers not in _NC_CACHE:
        _NC_CACHE[_n_layers] = Builder(_n_layers).build()
    nc = _NC_CACHE[_n_layers]
    shared = {
        "ngain": f(norm_gain),
        "a_w_in": f(a_w_in),
        "a_vg": f(np.asarray(a_v_gain).reshape(2, 16, P).transpose(0, 2, 1)),
        "a_wsT": f(np.asarray(a_w_s).transpose(0, 3, 1, 2)),
        "a_bs": f(np.asarray(a_b_s).reshape(2, 1, 1024)),
        "a_w_out": f(a_w_out),
        "b_w_in": f(np.asarray(b_w_in)[0]),
        "b_qkg": f(np.stack([np.asarray(b_q_gain)[0], np.asarray(b_k_gain)[0]], axis=1).reshape(1, 768)),
        "b_w_out": f(np.asarray(b_w_out)[0]),
        "c_w_in": f(np.asarray(c_w_in)[0]),
        "c_w_grp": f(np.asarray(c_w_grp)[0]),
        "c_sc": f(np.asarray(c_scale)[0].reshape(16, P).T),
        "c_w_out": f(np.asarray(c_w_out)[0]),
        "t_ident": np.eye(P, dtype=np.float32),
    }
    in_maps = []
    for core in range(NCORE):
        b, q = divmod(core, 4)
        xe = np.zeros((NTOK, D), np.float32)
        lo = q * OWN - HALO
        src_lo = max(lo, 0)
        xe[src_lo - lo:] = x[b, src_lo:q * OWN + OWN]
        cos, sin, kv, mask, icnt, onesv = _tables(core)
        m = dict(shared)
        m.update({"xe": xe.reshape(NCH, P, D), "t_cos": cos, "t_sin": sin, "t_kv": kv, "t_mask": mask, "t_icnt": icnt, "t_ones": onesv})
        in_maps.append(m)
    res = run_bass_kernel_spmd(nc, in_maps, core_ids=list(range(NCORE)))
    out = np.zeros((2, SEQ, D), np.float32)
    for core in range(NCORE):
        b, q = divmod(core, 4)
        out[b, q * OWN:(q + 1) * OWN] = res.results[core]["out"]
    return out
```

```python
import numpy as np
import ml_dtypes
import concourse.bass as bass
import concourse.mybir as mybir
from concourse.bass_utils import run_bass_kernel_spmd

F32 = mybir.dt.float32
BF16 = mybir.dt.bfloat16
AF = mybir.ActivationFunctionType
ALU = mybir.AluOpType
AX = mybir.AxisListType

P = 128
D = 1024
KC = 8
SEQ = 8192
NCORE = 8
OWN = 2048
HALO = 2176
NTOK = OWN + HALO
NCH = NTOK // P
CH0 = 16
NKEEP = NCH - CH0
EPS = 1e-6
NSLOT = 8


class Sync:
    def __init__(self, nc, stack):
        self.nc = nc
        self.h = {"pe": nc.tensor, "act": nc.scalar, "dve": nc.vector,
                  "pool": nc.gpsimd, "sp": nc.sync}
        self.sem = {}
        self.cnt = {}
        self.seen = {}
        for e in self.h:
            self.sem[e] = stack.enter_context(nc.semaphore("s_" + e))
            self.cnt[e] = 0
            self.seen[e] = {}
        self.dsem = {}
        self.dcnt = {}
        self.dnext = {}
        for q in ("sp", "pool", "act"):
            self.dnext[q] = 0
            for s in range(NSLOT):
                k = ("d", q, s)
                self.sem[k] = stack.enter_context(nc.semaphore("d_%s%d" % (q, s)))
                self.dcnt[k] = 0
        self.last_w = {}
        self.readers = {}
        self.alias = {}
        self.n_wait = 0
        self.n_ins = 0

    def _exp(self, keys):
        out = []
        for k in keys:
            if k in self.alias:
                out.extend(self.alias[k])
            else:
                out.append(k)
        return out

    def _wait(self, eng, sig):
        k, v = sig
        if v <= 0:
            return
        if self.seen[eng].get(k, 0) >= v:
            return
        self.h[eng].wait_ge(self.sem[k], v)
        self.seen[eng][k] = v
        self.n_wait += 1

    def _deps(self, eng, reads, writes):
        for r in reads:
            w = self.last_w.get(r)
            if w is not None and not (w[0] == eng and eng == "pe"):
                self._wait(eng, w)
            if r == "pT" or (isinstance(r, tuple) and r[0] == "ps"):
                for rd in self.readers.get(r, ()):
                    if rd[0] != eng:
                        self._wait(eng, rd)
        for wk in writes:
            w = self.last_w.get(wk)
            if w is not None and not (w[0] == eng and eng == "pe"):
                self._wait(eng, w)
            for rd in self.readers.get(wk, ()):
                if not (rd[0] == eng and eng == "pe"):
                    self._wait(eng, rd)

    def _record(self, sig, reads, writes):
        for r in reads:
            self.readers.setdefault(r, []).append(sig)
        for wk in writes:
            self.last_w[wk] = sig
            self.readers[wk] = []

    def op(self, eng, fn, reads=(), writes=(), signal=True):
        reads, writes = self._exp(reads), self._exp(writes)
        self._deps(eng, reads, writes)
        ins = fn(self.h[eng])
        self.n_ins += 1
        if signal:
            ins.then_inc(self.sem[eng], 1)
            self.cnt[eng] += 1
            sig = (eng, self.cnt[eng])
        else:
            sig = (eng, self.cnt[eng] + 1)
        self._record(sig, reads, writes)
        return ins

    def dma(self, q, out, in_, reads=(), writes=()):
        reads, writes = self._exp(reads), self._exp(writes)
        self._deps(q, reads, writes)
        s = self.dnext[q] % NSLOT
        self.dnext[q] += 1
        k = ("d", q, s)
        self._wait(q, (k, self.dcnt[k]))
        ins = self.h[q].dma_start(out=out, in_=in_)
        ins.then_inc(self.sem[k], 16)
        self.dcnt[k] += 16
        self.n_ins += 1
        sig = (k, self.dcnt[k])
        self._record(sig, reads, writes)
        return sig

    def wait_all(self, eng, keys):
        for k in keys:
            w = self.last_w.get(k)
            if w is not None:
                self._wait(eng, w)

    def barrier(self):
        sigs = [(e, self.cnt[e]) for e in self.h] + [(k, v) for k, v in self.dcnt.items()]
        for e in self.h:
            for s in sigs:
                if s[0] != e:
                    self._wait(e, s)
        self.last_w = {}
        self.readers = {}


GROUPS = ((1, 1920), (4, 1536), (16, 0))


def b_blocks():
    out = {}
    i = 0
    for g, (dil, base) in enumerate(GROUPS):
        m_tot = (NTOK - base) // dil
        nblk = -(-m_tot // P)
        for r in range(dil):
            for b in range(nblk):
                nb = min(P, m_tot - P * b)
                out[(g, r, b)] = (i, base + r + dil * P * b, nb, dil)
                i += 1
    return out, i


BLK, NBLK = b_blocks()


class Builder:
    def __init__(self, n_layers=4):
        self.n_layers = n_layers
        self.nc = bass.Bass("TRN2", target_bir_lowering=False)
        nc = self.nc
        di = lambda name, shape: nc.dram_tensor(name, shape, F32, kind="ExternalInput").ap()
        self.xe = di("xe", [NCH, P, D])
        self.ngain = di("ngain", [4, D])
        self.a_w_in = di("a_w_in", [2, D, 6144])
        self.a_vg = di("a_vg", [2, P, 16])
        self.a_wsT = di("a_wsT", [2, P, 8, P])
        self.a_bs = di("a_bs", [2, 1, 1024])
        self.a_w_out = di("a_w_out", [2, 2048, D])
        self.b_w_in = di("b_w_in", [D, 10240])
        self.b_qkg = di("b_qkg", [1, 768])
        self.b_w_out = di("b_w_out", [D, D])
        self.c_w_in = di("c_w_in", [D, 4096])
        self.c_w_grp = di("c_w_grp", [4, 512, 512])
        self.c_sc = di("c_sc", [P, 16])
        self.c_w_out = di("c_w_out", [2048, D])
        self.t_cos = di("t_cos", [P, NBLK, 16])
        self.t_sin = di("t_sin", [P, NBLK, 16])
        self.t_kv = di("t_kv", [P, NBLK])
        self.t_ones = di("t_ones", [P, 6, P])
        self.t_mask = di("t_mask", [P, 2, P])
        self.t_ident = di("t_ident", [P, P])
        self.t_icnt = di("t_icnt", [1, 64])
        self.out = nc.dram_tensor("out", [OWN, D], F32, kind="ExternalOutput").ap()
        self.wbf = {
            "c_w_in": nc.dram_tensor("wbf_c_in", [D, 4096], BF16, kind="Internal").ap(),
            "c_w_out": nc.dram_tensor("wbf_c_out", [2048, D], BF16, kind="Internal").ap(),
            "a_w_in1": nc.dram_tensor("wbf_a_in1", [D, 6144], BF16, kind="Internal").ap(),
            "a_w_out1": nc.dram_tensor("wbf_a_out1", [2048, D], BF16, kind="Internal").ap(),
        }
        self.h1s = nc.dram_tensor("h1s", [P, KC, NTOK], BF16, kind="Internal").ap()
        self.xpark = nc.dram_tensor("xpark", [NKEEP, P, D], F32, kind="Internal").ap()

    def sb(self, st, name, shape, dt):
        self.uid = getattr(self, "uid", 0) + 1
        return st.enter_context(self.nc.sbuf_tensor("sb%d_%s" % (self.uid, name), shape, dt))

    def setup_common(self, st):
        nc, S = self.nc, self.S
        self.ident = self.sb(st, "ident", [P, P], BF16)
        self.ones = self.sb(st, "ones", [P, P], BF16)
        self.nhalf = self.sb(st, "nhalf", [P, 8], F32)
        self.small = self.sb(st, "small", [P, 256], F32)
        self.small_i = 0
        self.junks = [self.sb(st, "junk%d" % i, [P, D], BF16) for i in range(3)]
        self.junk_i = 0
        self.ring = [self.sb(st, "ring%d" % i, [P, 4096], BF16) for i in range(self.R)]
        for i in range(self.R):
            S.alias[("w", i)] = [("w", i, t) for t in range(3)]
        self.mask = self.sb(st, "mask", [P, 2, P], F32)
        S.dma("sp", self.mask[:], self.t_mask, writes=["mask"])
        S.dma("pool", self.ident[:], self.t_ident, writes=["ident"])
        S.op("dve", lambda e: e.memset(self.ones[:], 1.0), writes=["ones"])
        S.op("dve", lambda e: e.memset(self.nhalf[:], -0.5), writes=["nhalf"])
        S.op("dve", lambda e: e.memset(self.small[:], 1.0), writes=[("sm", j) for j in range(256)])

    def junk(self, n):
        i = self.junk_i % len(self.junks)
        self.junk_i += 1
        return self.junks[i][:, 0:n], ("junk", i)

    def smallcol(self, n=1):
        if self.small_i + n > 256:
            self.small_i = 0
        i = self.small_i
        self.small_i += n
        return self.small[:, i:i + n], [("sm", j) for j in range(i, i + n)]

    def wplan(self, units):
        self.units = units
        self.w_issued = 0
        self.w_used = 0

    def wnext(self, tag):
        S = self.S
        i = self.w_used
        assert self.units[i][0] == tag, (i, self.units[i][0], tag)
        while self.w_issued < min(i + self.R - 1, len(self.units)):
            j = self.w_issued
            _, src, shape = self.units[j]
            n = int(np.prod(shape[1:]))
            dst = self.ring[j % self.R][:, 0:n]
            if len(shape) == 3:
                dst = dst.rearrange("p (a b) -> p a b", a=shape[1])
            elif len(shape) == 4:
                dst = dst.rearrange("p (a b c) -> p a b c", a=shape[1], b=shape[2])
            if len(shape) == 4:
                for t in range(shape[2]):
                    S.dma("pool", dst[:, :, t, :], src[:, :, t, :], writes=[("w", j % self.R, t)])
            else:
                q = "sp" if src.dtype == BF16 else "pool"
                S.dma(q, dst, src, writes=[("w", j % self.R)])
            self.w_issued += 1
        self.w_used += 1
        _, src, shape = self.units[i]
        n = int(np.prod(shape[1:]))
        t = self.ring[i % self.R][:, 0:n]
        if len(shape) == 3:
            t = t.rearrange("p (a b) -> p a b", a=shape[1])
        elif len(shape) == 4:
            t = t.rearrange("p (a b c) -> p a b c", a=shape[1], b=shape[2])
        return t, ("w", i % self.R)

    @staticmethod
    def colblock(w, c0, n=512):
        return w.rearrange("(k p) n -> p k n", p=P)[:, :, c0:c0 + n], [P, KC, n]

    def rstd_of(self, ss_ap, ss_keys, inv_n):
        S = self.S
        n = ss_ap.shape[1]
        ms, mk = self.smallcol(n)
        rs, rk = self.smallcol(n)
        S.op("dve", lambda e: e.tensor_scalar(out=ms, in0=ss_ap, scalar1=inv_n, scalar2=EPS, op0=ALU.mult, op1=ALU.add),
             reads=ss_keys, writes=mk)
        S.op("pool", lambda e: e.tensor_tensor(out=rs, in0=ms, in1=self.nhalf[:, 0:n], op=ALU.pow),
             reads=mk + ["nhalf"], writes=rk)
        return rs, rk

    def psnext(self):
        i = self.ps_i % len(self.psb)
        self.ps_i += 1
        return self.psb[i], ("ps", i)

    def hn_prep(self, x_ap, xkeys, gbc, hn, hk):
        S = self.S
        ss, sk = self.smallcol(1)
        jt, jk = self.junk(D)
        S.op("act", lambda e: e.activation(out=jt, in_=x_ap, func=AF.Square, accum_out=ss),
             reads=xkeys, writes=[jk] + sk)
        rs, rk = self.rstd_of(ss, sk, 1.0 / D)
        S.op("dve", lambda e: e.scalar_tensor_tensor(out=hn[:], in0=x_ap, scalar=rs[:, 0:1], in1=gbc, op0=ALU.mult, op1=ALU.mult),
             reads=xkeys + rk + ["gbc"], writes=[hk])

    def hn_trans(self, hn, hk, dst, dkeys):
        S = self.S
        for k in range(KC):
            S.op("pe", lambda e: e.transpose(self.pT[:, k * P:(k + 1) * P], hn[:, k * P:(k + 1) * P], self.ident[:]),
                 reads=[hk, "ident"], writes=["pT"], signal=(k == KC - 1))
        S.op("act", lambda e: e.activation(out=dst, in_=self.pT[:].rearrange("p (k t) -> p k t", k=KC), func=AF.Copy),
             reads=["pT"], writes=dkeys)

    def make_hT(self, x_ap, xkeys, gbc, dst, dkeys):
        hn = self.hn[self.hn_i % len(self.hn)]
        hk = ("hn", self.hn_i % len(self.hn))
        self.hn_i += 1
        self.hn_prep(x_ap, xkeys, gbc, hn, hk)
        self.hn_trans(hn, hk, dst, dkeys)

    def out_proj(self, yT, ykey_fn, nf, chunks_x, wtag, nparts=4):
        S = self.S
        wd = D // nparts
        for part in range(nparts):
            wo, wk = self.wnext(wtag)
            for ci, (x_ap, xkeys) in enumerate(chunks_x):
                bank, bk = self.psnext()
                for f in range(nf):
                    S.op("pe", lambda e: e.matmul(bank[:, 0:wd], lhsT=yT[:, f, ci * P:(ci + 1) * P], rhs=wo[:, f, :],
                                                  start=(f == 0), stop=(f == nf - 1)),
                         reads=[ykey_fn(f), wk], writes=[bk], signal=(f == nf - 1))
                xs = x_ap[:, part * wd:(part + 1) * wd]
                S.op("dve", lambda e: e.tensor_tensor(out=xs, in0=bank[:, 0:wd], in1=xs, op=ALU.add),
                     reads=[bk] + xkeys, writes=xkeys)

    def a_units(self, j, nblocks):
        if j == 1 and self.n_layers >= 4:
            w_in = self.wbf["a_w_in1"]
            wo = self.wbf["a_w_out1"].rearrange("(f p) n -> p f n", p=P)
        else:
            w_in = self.a_w_in[j]
            wo = self.a_w_out[j].rearrange("(f p) n -> p f n", p=P)
        u = []
        for _ in range(nblocks):
            for jv in range(4):
                u.append(("a_v",) + self.colblock(w_in, 2048 + jv * 512))
            for q in range(4):
                u.append(("a_u",) + self.colblock(w_in, q * 512))
                u.append(("a_z",) + self.colblock(w_in, 4096 + q * 512))
            for qq in range(4):
                u.append(("a_o", wo[:, :, qq * 256:(qq + 1) * 256], [P, 16, 256]))
        return u

    def layer_A(self, st, li, j, blocks, get_x, put_x):
        S = self.S
        sb = lambda name, shape, dt: self.sb(st, name, shape, dt)
        gbc = sb("a_gbc", [P, D], F32)
        S.dma("sp", gbc[:], self.ngain[li:li + 1, :].partition_broadcast(P), writes=["gbc"])
        bsb = sb("a_bsb", [P, 8, P], F32)
        S.dma("sp", bsb[:].rearrange("p g i -> p (g i)"), self.a_bs[j].partition_broadcast(P), writes=["bsb"])
        wsT = sb("a_wsT", [P, 8, P], F32)
        S.dma("sp", wsT[:], self.a_wsT[j], writes=["wsT"])
        S.op("dve", lambda e: e.tensor_tensor(out=wsT[:], in0=wsT[:], in1=self.mask[:, 1, :].unsqueeze(1).to_broadcast([P, 8, P]),
                                              op=ALU.mult), reads=["wsT", "mask"], writes=["wsT"])
        vg = sb("a_vg", [P, 16], F32)
        S.dma("sp", vg[:], self.a_vg[j], writes=["vg"])
        self.deferred = []
        hTs = [sb("a_hT%d" % i, [P, KC, 512], BF16) for i in range(2)]
        vraw = [sb("a_vraw%d" % i, [P, 2048], BF16) for i in range(4)]
        wss = [sb("a_wss%d" % i, [P, 8, P], BF16) for i in range(4)]
        yT = sb("a_yT", [P, 16, 512], BF16)
        szt = [sb("a_sz%d" % i, [P, 512], F32) for i in range(2)]
        t1t = [sb("a_t1%d" % i, [P, 512], F32) for i in range(2)]
        hnb = [sb("a_hnb%d" % i, [P, D], BF16) for i in range(4)]

        def prep(bi):
            xs_ = []
            for ci, c in enumerate(blocks[bi]):
                x_ap, xkeys = get_x(c)
                xs_.append((x_ap, xkeys))
                self.hn_prep(x_ap, xkeys, gbc[:], hnb[ci], ("hnb", ci))
            return xs_

        def trans(bi):
            for ci in range(len(blocks[bi])):
                self.hn_trans(hnb[ci], ("hnb", ci), hTs[bi % 2][:, :, ci * P:(ci + 1) * P], [("hT", bi % 2, ci)])

        xs_next = prep(0)
        trans(0)
        for bi, cl in enumerate(blocks):
            n = len(cl)
            N = n * P
            hT = hTs[bi % 2]
            xs = xs_next
            if bi + 1 < len(blocks):
                xs_next = prep(bi + 1)
            hkeys = [("hT", bi % 2, ci) for ci in range(n)]
            ssa, ssk = self.smallcol(16)
            for jv in range(4):
                wt, wk = self.wnext("a_v")
                for ci in range(n):
                    bank, bk = self.psnext()
                    for k in range(KC):
                        S.op("pe", lambda e: e.matmul(bank[:, 0:512], lhsT=hT[:, k, ci * P:(ci + 1) * P], rhs=wt[:, k, :],
                                                      start=(k == 0), stop=(k == KC - 1)),
                             reads=[hkeys[ci], wk], writes=[bk], signal=(k == KC - 1))
                    col = ci * 4 + jv
                    jt, jk = self.junk(512)
                    S.op("act", lambda e: e.activation(out=jt, in_=bank[:, 0:512], func=AF.Square,
                                                       accum_out=ssa[:, col:col + 1]),
                         reads=[bk], writes=[jk, ssk[col]])
                    S.op("dve", lambda e: e.tensor_copy(out=vraw[ci][:, jv * 512:(jv + 1) * 512], in_=bank[:, 0:512]),
                         reads=[bk], writes=[("vraw", ci, jv)])
            st4, st4k = self.smallcol(4)
            S.op("dve", lambda e: e.tensor_reduce(out=st4[:, 0:n], in_=ssa[:, 0:4 * n].rearrange("p (c j) -> p c j", j=4), axis=AX.X, op=ALU.add),
                 reads=ssk[0:4 * n], writes=st4k[0:n])
            rs4, rs4k = self.rstd_of(st4[:, 0:n], st4k[0:n], 1.0 / 2048)
            for ci in range(n):
                S.op("dve", lambda e: e.tensor_scalar(out=wss[ci][:], in0=wsT[:], scalar1=rs4[:, ci:ci + 1], scalar2=None, op0=ALU.mult),
                     reads=["wsT"] + rs4k, writes=[("wss", ci)])
            for fn in self.deferred:
                fn()
            self.deferred = []
            for q in range(4):
                wu, wuk = self.wnext("a_u")
                wz, wzk = self.wnext("a_z")
                for fi in range(4):
                    f = 4 * q + fi
                    g = f // 2
                    pu, puk = self.psnext()
                    pz, pzk = self.psnext()
                    pm, pmk = self.psnext()
                    for k in range(KC):
                        S.op("pe", lambda e: e.matmul(pu[:, 0:N], lhsT=wu[:, k, fi * P:(fi + 1) * P], rhs=hT[:, k, 0:N],
                                                      start=(k == 0), stop=(k == KC - 1)),
                             reads=hkeys + [wuk], writes=[puk], signal=(k == KC - 1))
                    for k in range(KC):
                        S.op("pe", lambda e: e.matmul(pz[:, 0:N], lhsT=wz[:, k, fi * P:(fi + 1) * P], rhs=hT[:, k, 0:N],
                                                      start=(k == 0), stop=(k == KC - 1)),
                             reads=hkeys + [wzk], writes=[pzk], signal=(k == KC - 1))
                    for ci in range(n):
                        S.op("pe", lambda e: e.matmul(pm[:, ci * P:(ci + 1) * P], lhsT=vraw[ci][:, f * P:(f + 1) * P],
                                                      rhs=wss[ci][:, g, :], start=True, stop=True),
                             reads=[("vraw", ci, f // 4), ("wss", ci)], writes=[pmk], signal=(ci == n - 1))
                    sz = szt[f % 2]
                    t1 = t1t[f % 2]
                    S.op("act", lambda e: e.activation(out=sz[:, 0:N], in_=pz[:, 0:N], func=AF.Silu),
                         reads=[pzk], writes=[("sz", f % 2)])
                    S.op("dve", lambda e: e.scalar_tensor_tensor(
                        out=t1[:, 0:N].rearrange("p (c i) -> p c i", c=n), in0=pm[:, 0:N].rearrange("p (c i) -> p c i", c=n),
                        scalar=vg[:, f:f + 1], in1=bsb[:, g, :].unsqueeze(1).to_broadcast([P, n, P]),
                        op0=ALU.mult, op1=ALU.add),
                        reads=[pmk, "vg", "bsb"], writes=[("t1", f % 2)])
                    S.op("dve", lambda e: e.tensor_tensor(out=t1[:, 0:N], in0=pu[:, 0:N], in1=t1[:, 0:N], op=ALU.mult),
                         reads=[puk, ("t1", f % 2)], writes=[("t1", f % 2)])
                    S.op("dve", lambda e: e.tensor_tensor(out=yT[:, f, 0:N], in0=t1[:, 0:N], in1=sz[:, 0:N], op=ALU.mult),
                         reads=[("t1", f % 2), ("sz", f % 2)], writes=[("yT", f)])
            if bi + 1 < len(blocks):
                trans(bi + 1)
            self.out_proj(yT, lambda f: ("yT", f), 16, xs, "a_o")
            for ci, c in enumerate(cl):
                put_x(c, xs[ci][0], xs[ci][1])
        for fn in self.deferred:
            fn()
        self.deferred = []

    def b_units(self):
        w = self.b_w_in.rearrange("(k p) (j n) -> p k j n", p=P, n=P)
        u = []
        for h in range(8):
            for g in range(3):
                j0 = g * 8 + h
                u.append(("b_qkv", w[:, :, j0:j0 + 49:24, :], [P, KC, 3, P]))
            u.append(("b_z", w[:, :, 72 + h, :], [P, KC, P]))
        wo = self.b_w_out.rearrange("(f p) n -> p f n", p=P)
        for half in range(2):
            u.append(("b_o", wo[:, :, half * 512:(half + 1) * 512], [P, 8, 512]))
        return u

    def layer_B(self, st):
        S = self.S
        sb = lambda name, shape, dt: self.sb(st, name, shape, dt)
        NQ = NTOK - 2048
        hTB = sb("b_hT", [P, KC, NTOK], BF16)
        for k in range(KC):
            S.dma("sp", hTB[:, k, :], self.h1s[:, k, :], writes=["hTB"])
        cos = sb("b_cos", [P, NBLK, 16], F32)
        sin = sb("b_sin", [P, NBLK, 16], F32)
        kv = sb("b_kv", [P, NBLK], F32)
        S.dma("sp", cos[:], self.t_cos, writes=["cos"])
        S.dma("sp", sin[:], self.t_sin, writes=["sin"])
        S.dma("sp", kv[:], self.t_kv, writes=["kv"])
        onesv = sb("b_onesv", [P, 6, P], BF16)
        S.dma("pool", onesv[:], self.t_ones, writes=["onesv"])
        qkg = sb("b_qkg", [P, 3, 2, P], F32)
        S.dma("sp", qkg[:].rearrange("p g j d -> p (g j d)"), self.b_qkg.partition_broadcast(P), writes=["qkg"])
        S.op("dve", lambda e: e.tensor_scalar(out=qkg[:], in0=qkg[:], scalar1=float(np.sqrt(128.0)), scalar2=None, op0=ALU.mult),
             reads=["qkg"], writes=["qkg"])
        eps128 = sb("b_eps", [P, 8], F32)
        S.op("dve", lambda e: e.memset(eps128[:], 128.0 * EPS), writes=["eps128"])
        maskb = sb("b_maskb", [P, 2, P], BF16)
        S.op("dve", lambda e: e.tensor_copy(out=maskb[:], in_=self.mask[:]), reads=["mask"], writes=["maskb"])
        OD = sb("b_OD", [P, 2, NQ], F32)
        yTB = sb("b_yT", [P, 8, NQ], BF16)
        NQK, NRT, NV, NT, NE = 8, 3, 12, 5, 5
        qkall = sb("b_qkall", [P, NQK, 2, P], BF16)
        S.op("dve", lambda e: e.memset(qkall[:], 0.0), writes=[(("qk", i), j) for i in range(NQK) for j in range(2)])
        rtmp = [sb("b_rt%d" % i, [P, 4, 2, 2, 16], F32) for i in range(NRT)]
        pairst = {}
        Vt = [sb("b_V%d" % i, [P, P], BF16) for i in range(NV)]
        qkT = [sb("b_qkT%d" % i, [P, 2, P], BF16) for i in range(NT)]
        Et = [sb("b_E%d" % i, [P, 2, P], BF16) for i in range(NE)]
        PTt = [sb("b_PT%d" % i, [P, 2, P], BF16) for i in range(NE)]
        sqj = [sb("b_sq%d" % i, [P, P], BF16) for i in range(4)]
        szt = [sb("b_sz%d" % i, [P, 512], F32) for i in range(2)]
        xp = [sb("b_xp%d" % i, [P, D], F32) for i in range(2)]
        scale = 1.0 / float(np.sqrt(128.0))
        cnt = {"s": 0, "o": 0, "sq": 0}
        wcur = {}

        def stage_P(bs, n):
            nb, start, dil = bs["nb"], bs["start"], bs["dil"]
            bank, bk = self.psb[n % 4], ("ps", n % 4)
            bs["bank"], bs["bk"] = bank, bk
            if bs["newg"]:
                wcur["w"] = self.wnext("b_qkv")
            wq, wqk = wcur["w"]
            stop_tok = start + dil * (nb - 1) + 1
            j0 = bs["j0"]
            for k in range(KC):
                S.op("pe", lambda e: e.matmul(bank[:nb, j0 * P:384], lhsT=hTB[:, k, start:stop_tok:dil],
                                              rhs=wq[:, k, j0:3, :].rearrange("p a b -> p (a b)"),
                                              start=(k == 0), stop=(k == KC - 1)),
                     reads=["hTB", wqk], writes=[bk], signal=(k == KC - 1))

        def stage_E(bs, n):
            nb, idx, g = bs["nb"], bs["idx"], bs["g"]
            bank, bk = bs["bank"], bs["bk"]
            if n % 2 == 0:
                pairst["ss"] = self.smallcol(4)
            ssa, ska = pairst["ss"]
            o = 2 * (n % 2)
            ss, sk = ssa[:, o:o + 2], ska[o:o + 2]
            for j in range(bs["j0"], 2):
                jt = sqj[cnt["sq"] % 4]
                jk = ("sqj", cnt["sq"] % 4)
                cnt["sq"] += 1
                S.op("act", lambda e: e.activation(out=jt[:nb, :], in_=bank[:nb, j * P:(j + 1) * P], func=AF.Square,
                                                   accum_out=ss[:nb, j:j + 1]),
                     reads=[bk], writes=[jk, sk[j]])
            V = Vt[n % NV]
            Vk = ("V", n % NV)
            S.op("act", lambda e: e.activation(out=V[:nb, :], in_=bank[:nb, 2 * P:3 * P], func=AF.Copy, scale=kv[:nb, idx:idx + 1]),
                 reads=[bk, "kv"], writes=[Vk])
            bs.update(V=V, Vk=Vk, ssa=ssa, ska=ska, o=o)

        def stage_E2(blks, n0, pi):
            ssa, ska = blks[0]["ssa"], blks[0]["ska"]
            ms, mk = self.smallcol(4)
            rs, rk = self.smallcol(4)
            S.op("pool", lambda e: e.tensor_tensor(out=ms, in0=ssa, in1=eps128[:, 0:4], op=ALU.add), reads=ska + ["eps128"], writes=mk)
            S.op("pool", lambda e: e.tensor_tensor(out=rs, in0=ms, in1=self.nhalf[:, 0:4], op=ALU.pow), reads=mk + ["nhalf"], writes=rk)
            for i, bs in enumerate(blks):
                n = n0 + i
                nb, g, bank, bk, o = bs["nb"], bs["g"], bs["bank"], bs["bk"], bs["o"]
                qk = qkall[:, n % NQK, :, :]
                qkk = ("qk", n % NQK)
                for j in range(bs["j0"], 2):
                    S.op("dve", lambda e: e.scalar_tensor_tensor(out=qk[:nb, j, :], in0=bank[:nb, j * P:(j + 1) * P],
                                                                  scalar=rs[:nb, o + j:o + j + 1], in1=qkg[:nb, g, j, :],
                                                                  op0=ALU.mult, op1=ALU.mult),
                         reads=[bk] + rk + ["qkg"], writes=[(qkk, j)])
                bs.update(qk=qk, qkk=qkk)
            reng = "dve" if pi % 3 else "pool"
            rt = rtmp[pi % NRT]
            rtk = ("rt", pi % NRT)
            if len(blks) == 2 and blks[0]["nb"] == P and blks[1]["nb"] == P:
                groups = [(blks, qkall[:, n0 % NQK:n0 % NQK + 2, :, :], P, 2)]
            else:
                groups = [([bs], qkall[:, (n0 + i) % NQK:(n0 + i) % NQK + 1, :, :], bs["nb"], 1) for i, bs in enumerate(blks)]
            for bl, qv, nb, m in groups:
                idx = bl[0]["idx"]
                keys = [(b_["qkk"], j) for b_ in bl for j in range(2)]
                x1 = qv[:nb, :, :, 0:16]
                x2 = qv[:nb, :, :, 16:32]
                cb = cos[:nb, idx:idx + m, :].unsqueeze(2).to_broadcast([nb, m, 2, 16])
                sbb = sin[:nb, idx:idx + m, :].unsqueeze(2).to_broadcast([nb, m, 2, 16])
                tv = lambda t: rt[:nb, t, 0:m, :, :]
                for t, (a_, b_) in enumerate(((x1, cb), (x2, sbb), (x2, cb), (x1, sbb))):
                    S.op(reng, lambda e: e.tensor_tensor(out=tv(t), in0=a_, in1=b_, op=ALU.mult),
                         reads=keys + ["cos", "sin"], writes=[(rtk, t)])
                S.op(reng, lambda e: e.tensor_tensor(out=x1, in0=tv(0), in1=tv(1), op=ALU.subtract),
                     reads=[(rtk, 0), (rtk, 1)], writes=keys)
                S.op(reng, lambda e: e.tensor_tensor(out=x2, in0=tv(2), in1=tv(3), op=ALU.add),
                     reads=[(rtk, 2), (rtk, 3)], writes=keys)

        def stage_T(bs, n):
            nb, qk, qkk = bs["nb"], bs["qk"], bs["qkk"]
            j0 = bs["j0"]
            for j in range(j0, 2):
                S.op("pe", lambda e: e.transpose(self.pT[:, j * P:j * P + nb], qk[:nb, j, :], self.ident[:nb, :nb]),
                     reads=[(qkk, 0), (qkk, 1), "ident"], writes=["pT"], signal=(j == 1))
            T = qkT[n % NT]
            Tk = ("qkT", n % NT)
            S.op("act", lambda e: e.activation(out=T[:, j0:2, 0:nb], in_=self.pT[:, 0:2 * P].rearrange("p (j t) -> p j t", j=2)[:, j0:2, 0:nb],
                                               func=AF.Copy),
                 reads=["pT"], writes=[Tk])
            bs.update(T=T, Tk=Tk)

        def stage_S(cur, prev):
            nq = cur["nb"]
            si = cnt["s"]
            cnt["s"] += 1
            sbank, sbk = self.psb[4], ("ps", 4)
            for t, kb in enumerate((prev, cur)):
                nk = kb["nb"]
                S.op("pe", lambda e: e.matmul(sbank[:nk, t * P:t * P + nq], lhsT=kb["T"][:, 1, 0:nk], rhs=cur["T"][:, 0, 0:nq],
                                              start=True, stop=True),
                     reads=[kb["Tk"], cur["Tk"]], writes=[sbk], signal=(t == 1))
            E = Et[si % NE]
            PT = PTt[si % NE]
            Ek, PTk = ("E", si % NE), ("PT", si % NE)
            if nq == P:
                S.op("act", lambda e: e.activation(out=E[:].rearrange("p a b -> p (a b)"), in_=sbank[:, 0:2 * P], func=AF.Exp, scale=scale),
                     reads=[sbk], writes=[Ek])
                S.op("dve", lambda e: e.tensor_tensor(out=PT[:], in0=E[:], in1=maskb[:], op=ALU.mult),
                     reads=[Ek, "maskb"], writes=[PTk])
            else:
                for t, kb in enumerate((prev, cur)):
                    nk = kb["nb"]
                    S.op("act", lambda e: e.activation(out=E[:nk, t, 0:nq], in_=sbank[:nk, t * P:t * P + nq], func=AF.Exp, scale=scale),
                         reads=[sbk], writes=[Ek])
                    S.op("dve", lambda e: e.tensor_tensor(out=PT[:nk, t, 0:nq], in0=E[:nk, t, 0:nq], in1=self.mask[:nk, t, 0:nq], op=ALU.mult),
                         reads=[Ek, "mask"], writes=[PTk])
            cur.update(PT=PT, PTk=PTk)

        def stage_O(cur, prev, first):
            nq = cur["nb"]
            oi = cnt["o"]
            cnt["o"] += 1
            obank, obk = self.psb[5 + oi % 2], ("ps", 5 + oi % 2)
            PT, PTk = cur["PT"], cur["PTk"]
            for t, kb in enumerate((prev, cur)):
                nk = kb["nb"]
                S.op("pe", lambda e: e.matmul(obank[:, 0:nq], lhsT=kb["V"][:nk, :], rhs=PT[:nk, t, 0:nq],
                                              start=(t == 0), stop=(t == 1)),
                     reads=[kb["Vk"], PTk], writes=[obk], signal=False)
            for t, kb in enumerate((prev, cur)):
                nk = kb["nb"]
                ov = self.ones[:nk, :] if kb["b"] >= 2 else onesv[:nk, 2 * kb["g"] + kb["b"], :]
                S.op("pe", lambda e: e.matmul(obank[:, P:P + nq], lhsT=ov, rhs=PT[:nk, t, 0:nq],
                                              start=(t == 0), stop=(t == 1)),
                     reads=["ones", "onesv", PTk], writes=[obk], signal=(t == 1))
            q0 = cur["start"] - 2048
            q1 = q0 + cur["dil"] * (nq - 1) + 1
            osl = OD[:, :, q0:q1:cur["dil"]]
            src = obank[:, 0:2 * P].rearrange("p (a b) -> p a b", a=2)[:, :, 0:nq]
            if first:
                S.op("act", lambda e: e.activation(out=osl, in_=src, func=AF.Copy), reads=[obk], writes=["OD"])
            else:
                S.op("dve", lambda e: e.tensor_tensor(out=osl, in0=src, in1=osl, op=ALU.add), reads=[obk, "OD"], writes=["OD"])

        conv = []
        if self.n_layers >= 3:
            conv += [(self.wbf["c_w_in"][i * P:(i + 1) * P, :], self.c_w_in[i * P:(i + 1) * P, :]) for i in range(8)]
            conv += [(self.wbf["c_w_out"][i * P:(i + 1) * P, :], self.c_w_out[i * P:(i + 1) * P, :]) for i in range(16)]
        if self.n_layers >= 4:
            conv += [(self.wbf["a_w_in1"][i * P:(i + 1) * P, :], self.a_w_in[1][i * P:(i + 1) * P, :]) for i in range(8)]
            conv += [(self.wbf["a_w_out1"][i * P:(i + 1) * P, :], self.a_w_out[1][i * P:(i + 1) * P, :]) for i in range(16)]
        for h in range(8):
            n_c = -(-len(conv) // 8)
            for ci_, (dst_, src_) in enumerate(conv[h * n_c:(h + 1) * n_c]):
                S.dma("pool", dst_, src_, writes=[("wconv", h, ci_)])
            blist = []
            for g in range(3):
                dil, base = GROUPS[g]
                nblk = -(-((NTOK - base) // dil) // P)
                for r in range(dil):
                    for b in range(nblk):
                        idx, start, nb, _ = BLK[(g, r, b)]
                        blist.append(dict(g=g, r=r, b=b, idx=idx, start=start, nb=nb, dil=dil, newg=(r == 0 and b == 0),
                                           j0=(1 if b == 0 else 0)))
            NB_ = len(blist)
            npair = 0
            for n in range(NB_ + 7):
                if n < NB_:
                    stage_P(blist[n], n)
                    stage_E(blist[n], n)
                if 0 <= n - 4 < NB_:
                    stage_T(blist[n - 4], n - 4)
                if 0 <= n - 5 < NB_ and blist[n - 5]["b"] >= 1:
                    stage_S(blist[n - 5], blist[n - 6])
                if 0 <= n - 7 < NB_ and blist[n - 7]["b"] >= 1:
                    stage_O(blist[n - 7], blist[n - 8], first=(blist[n - 7]["g"] == 0))
                if n < NB_ and n % 2 == 1:
                    stage_E2(blist[n - 1:n + 1], n - 1, npair)
                    npair += 1
                elif n == NB_ - 1:
                    stage_E2(blist[n:n + 1], n, npair)
                    npair += 1
            wz, wzk = self.wnext("b_z")
            Den = OD[:, 1, :]
            Oacc = OD[:, 0, :]
            S.op("dve", lambda e: e.tensor_scalar(out=Den, in0=Den, scalar1=1e-30, scalar2=None, op0=ALU.max),
                 reads=["OD"], writes=["OD"])
            S.op("dve", lambda e: e.reciprocal(out=Den, in_=Den), reads=["OD"], writes=["OD"])
            S.op("dve", lambda e: e.tensor_tensor(out=Oacc, in0=Oacc, in1=Den, op=ALU.mult),
                 reads=["OD"], writes=["OD"])
            for tb in range(5):
                n = min(512, NQ - tb * 512)
                zb, zbk = self.psnext()
                for k in range(KC):
                    S.op("pe", lambda e: e.matmul(zb[:, 0:n], lhsT=wz[:, k, :], rhs=hTB[:, k, 2048 + tb * 512:2048 + tb * 512 + n],
                                                  start=(k == 0), stop=(k == KC - 1)),
                         reads=["hTB", wzk], writes=[zbk], signal=(k == KC - 1))
                sz = szt[tb % 2]
                S.op("act", lambda e: e.activation(out=sz[:, 0:n], in_=zb[:, 0:n], func=AF.Silu), reads=[zbk], writes=[("sz", tb % 2)])
                S.op("dve", lambda e: e.tensor_tensor(out=yTB[:, h, tb * 512:tb * 512 + n], in0=OD[:, 0, tb * 512:tb * 512 + n],
                                                      in1=sz[:, 0:n], op=ALU.mult),
                     reads=["OD", ("sz", tb % 2)], writes=[("yTB", h)])
        wo0, wo0k = self.wnext("b_o")
        wo1, wo1k = self.wnext("b_o")
        for ci in range(NKEEP):
            x = xp[ci % 2]
            xk = ("xp", ci % 2)
            S.dma("sp", x[:], self.xpark[ci], reads=[("xpark", ci)], writes=[xk])
            for half, (wo, wok) in enumerate(((wo0, wo0k), (wo1, wo1k))):
                bank, bk = self.psnext()
                for hh in range(8):
                    S.op("pe", lambda e: e.matmul(bank[:, 0:512], lhsT=yTB[:, hh, ci * P:(ci + 1) * P], rhs=wo[:, hh, :],
                                                  start=(hh == 0), stop=(hh == 7)),
                         reads=[("yTB", hh), wok], writes=[bk], signal=(hh == 7))
                xs = x[:, half * 512:(half + 1) * 512]
                S.op("dve", lambda e: e.tensor_tensor(out=xs, in0=bank[:, 0:512], in1=xs, op=ALU.add), reads=[bk, xk], writes=[xk])
            S.dma("sp", self.xpark[ci], x[:], reads=[xk], writes=[("xpark", ci)])

    def c_units(self):
        w_in = self.wbf["c_w_in"]
        wo = self.wbf["c_w_out"].rearrange("(f p) n -> p f n", p=P)
        u = []
        for g in range(4):
            u.append(("c_x",) + self.colblock(w_in, g * 512))
        for _ in range(4):
            for g in range(4):
                u.append(("c_x",) + self.colblock(w_in, g * 512))
                u.append(("c_z",) + self.colblock(w_in, 2048 + g * 512))
            for qq in range(4):
                u.append(("c_o", wo[:, :, qq * 256:(qq + 1) * 256], [P, 16, 256]))
        return u

    def layer_C(self, st, X, blocks):
        S = self.S
        sb = lambda name, shape, dt: self.sb(st, name, shape, dt)
        gbc = sb("c_gbc", [P, D], F32)
        S.dma("sp", gbc[:], self.ngain[2:3, :].partition_broadcast(P), writes=["gbc"])
        csc = sb("c_sc", [P, 16], F32)
        S.dma("sp", csc[:], self.c_sc, writes=["csc"])
        icnt = sb("c_icnt", [P, 4, 16], F32)
        S.dma("sp", icnt[:].rearrange("p g i -> p (g i)"), self.t_icnt.partition_broadcast(P), writes=["icnt"])
        wgrp = sb("c_wgrp", [P, 4, 4, 512], BF16)
        for g in range(4):
            S.dma("pool", wgrp[:, g, :, :], self.c_w_grp[g].rearrange("(fi p) n -> p fi n", p=P), writes=[("wgrp", g)])
        self.hn = [sb("c_hn%d" % i, [P, D], BF16) for i in range(2)]
        self.hn_i = 0
        hTs = [sb("c_hT%d" % i, [P, KC, 512], BF16) for i in range(2)]
        yT = sb("c_yT", [P, 16, 512], BF16)
        dT = [sb("c_dT%d" % i, [P, 512], BF16) for i in range(4)]
        xcb = [sb("c_xcb%d" % i, [P, 528], F32) for i in range(2)]
        sab = [sb("c_sab%d" % i, [P, 528], F32) for i in range(2)]
        carry = sb("c_carry", [P, 16, 16], F32)
        fix = sb("c_fix", [P, 16], F32)
        szt = [sb("c_sz%d" % i, [P, 512], F32) for i in range(2)]
        hT16 = hTs[1][:, :, 0:P]
        self.make_hT(X[:, 0, :], [("X", 0)], gbc[:], hT16, [("hT", 1, 0)])
        for g in range(4):
            wx, wxk = self.wnext("c_x")
            for fi in range(4):
                f = 4 * g + fi
                bank, bk = self.psnext()
                for k in range(KC):
                    S.op("pe", lambda e: e.matmul(bank[:, 0:16], lhsT=wx[:, k, fi * P:(fi + 1) * P], rhs=hT16[:, k, 112:128],
                                                  start=(k == 0), stop=(k == KC - 1)),
                         reads=[("hT", 1, 0), wxk], writes=[bk], signal=(k == KC - 1))
                S.op("act", lambda e: e.activation(out=carry[:, f, :], in_=bank[:, 0:16], func=AF.Copy),
                     reads=[bk], writes=[("carry", f)])
        xi = 0
        hnb = [sb("c_hnb%d" % i, [P, D], BF16) for i in range(4)]

        def prep(bi):
            xs_ = []
            for ci, c in enumerate(blocks[bi]):
                x_ap, xkeys = X[:, c - CH0, :], [("X", c - CH0)]
                xs_.append((x_ap, xkeys))
                self.hn_prep(x_ap, xkeys, gbc[:], hnb[ci], ("hnb", ci))
            return xs_

        def trans(bi):
            for ci in range(len(blocks[bi])):
                self.hn_trans(hnb[ci], ("hnb", ci), hTs[bi % 2][:, :, ci * P:(ci + 1) * P], [("hT", bi % 2, ci)])

        xs_next = prep(0)
        trans(0)
        for bi, cl in enumerate(blocks):
            n = len(cl)
            N = n * P
            hT = hTs[bi % 2]
            xs = xs_next
            if bi + 1 < len(blocks):
                xs_next = prep(bi + 1)
            hkeys = [("hT", bi % 2, ci) for ci in range(n)]
            for g in range(4):
                w = 2 << g
                wx, wxk = self.wnext("c_x")
                wz, wzk = self.wnext("c_z")
                for fi in range(4):
                    f = 4 * g + fi
                    px, pxk = self.psnext()
                    for k in range(KC):
                        S.op("pe", lambda e: e.matmul(px[:, 0:N], lhsT=wx[:, k, fi * P:(fi + 1) * P], rhs=hT[:, k, 0:N],
                                                      start=(k == 0), stop=(k == KC - 1)),
                             reads=hkeys + [wxk], writes=[pxk], signal=(k == KC - 1))
                    xb = xcb[xi % 2]
                    xk = ("xcb", xi % 2)
                    xi += 1
                    S.op("act", lambda e: e.activation(out=xb[:, 16:16 + N], in_=px[:, 0:N], func=AF.Copy),
                         reads=[pxk], writes=[xk])
                    S.op("act", lambda e: e.activation(out=xb[:, 0:16], in_=carry[:, f, :], func=AF.Copy),
                         reads=[("carry", f)], writes=[xk])
                    S.op("act", lambda e: e.activation(out=carry[:, f, :], in_=xb[:, N:N + 16], func=AF.Copy),
                         reads=[xk], writes=[("carry", f)])
                    cur, ck = xb, xk
                    step = 1
                    si = 0
                    while step < w:
                        i0 = 2 * step - 1
                        nxt, nk = sab[si % 2], ("sab", si % 2)
                        si += 1
                        S.op("dve", lambda e: e.tensor_tensor(out=nxt[:, i0:16 + N], in0=cur[:, i0:16 + N],
                                                              in1=cur[:, i0 - step:16 + N - step], op=ALU.add),
                             reads=[ck], writes=[nk])
                        cur, ck = nxt, nk
                        step *= 2
                    S.op("dve", lambda e: e.scalar_tensor_tensor(out=dT[fi][:, 0:N], in0=cur[:, 16:16 + N], scalar=1.0 / w,
                                                                  in1=xb[:, 16:16 + N], op0=ALU.mult, op1=ALU.subtract),
                         reads=[ck, xk], writes=[("dT", fi)])
                    if bi == 0:
                        S.op("dve", lambda e: e.tensor_tensor(out=fix[:], in0=cur[:, 16:32], in1=icnt[:, g, :], op=ALU.mult),
                             reads=[ck, "icnt"], writes=["fix"])
                        S.op("dve", lambda e: e.tensor_tensor(out=dT[fi][:, 0:16], in0=fix[:], in1=xb[:, 16:32], op=ALU.subtract),
                             reads=["fix", xk, ("dT", fi)], writes=[("dT", fi)])
                for fo in range(4):
                    f = 4 * g + fo
                    pm, pmk = self.psnext()
                    pz, pzk = self.psnext()
                    for fi in range(4):
                        S.op("pe", lambda e: e.matmul(pm[:, 0:N], lhsT=wgrp[:, g, fi, fo * P:(fo + 1) * P], rhs=dT[fi][:, 0:N],
                                                      start=(fi == 0), stop=(fi == 3)),
                             reads=[("wgrp", g), ("dT", fi)], writes=[pmk], signal=(fi == 3))
                    for k in range(KC):
                        S.op("pe", lambda e: e.matmul(pz[:, 0:N], lhsT=wz[:, k, fo * P:(fo + 1) * P], rhs=hT[:, k, 0:N],
                                                      start=(k == 0), stop=(k == KC - 1)),
                             reads=hkeys + [wzk], writes=[pzk], signal=(k == KC - 1))
                    sz = szt[f % 2]
                    S.op("act", lambda e: e.activation(out=sz[:, 0:N], in_=pz[:, 0:N], func=AF.Silu),
                         reads=[pzk], writes=[("sz", f % 2)])
                    S.op("dve", lambda e: e.scalar_tensor_tensor(out=yT[:, f, 0:N], in0=pm[:, 0:N], scalar=csc[:, f:f + 1],
                                                                  in1=sz[:, 0:N], op0=ALU.mult, op1=ALU.mult),
                         reads=[pmk, "csc", ("sz", f % 2)], writes=[("yT", f)])
            if bi + 1 < len(blocks):
                trans(bi + 1)
            self.out_proj(yT, lambda f: ("yT", f), 16, xs, "c_o")

    def build(self):
        from contextlib import ExitStack
        nc = self.nc
        self.R = 4
        A_BLOCKS0 = [[0, 1, 2, 3], [4, 5, 6, 7], [8, 9, 10, 11], [12, 13, 14, 15], [16],
                     [17, 18, 19, 20], [21, 22, 23, 24], [25, 26, 27, 28], [29, 30, 31, 32]]
        OWN_BLOCKS = A_BLOCKS0[5:]
        units = self.a_units(0, len(A_BLOCKS0))
        if self.n_layers >= 2:
            units += self.b_units()
        if self.n_layers >= 3:
            units += self.c_units()
        if self.n_layers >= 4:
            units += self.a_units(1, len(OWN_BLOCKS))
        self.wplan(units)
        with ExitStack() as st:
            self.S = S = Sync(nc, st)
            self.st0 = st
            self.setup_common(st)
            self.psb = [st.enter_context(nc.psum_tensor("psb%d" % i, [P, 512], F32)) for i in range(7)]
            self.ps_i = 0
            self.pT = st.enter_context(nc.psum_tensor("pT", [P, 1024], BF16))
            last = self.n_layers
            with ExitStack() as l0:
                xin = [self.sb(l0, "xin%d" % i, [P, D], F32) for i in range(8)]
                gbc1 = self.sb(l0, "gbc1", [P, D], F32)
                S.dma("sp", gbc1[:], self.ngain[1:2, :].partition_broadcast(P), writes=["gbc1"])
                h1T = [self.sb(l0, "h1T%d" % i, [P, KC, P], BF16) for i in range(2)]
                state = {"i": 0, "slot": {}, "h": 0}
                h1n = [self.sb(l0, "h1n%d" % i, [P, D], BF16) for i in range(4)]

                def get_x(c):
                    s = state["i"] % 8
                    state["i"] += 1
                    state["slot"][c] = s
                    S.dma("sp", xin[s][:], self.xe[c], writes=[("xin", s)])
                    return xin[s][:], [("xin", s)]

                def put_x(c, ap, keys):
                    if last == 1:
                        if c >= 17:
                            S.dma("sp", self.out[(c - 17) * P:(c - 16) * P, :], ap, reads=keys)
                        return
                    if c >= CH0:
                        S.dma("sp", self.xpark[c - CH0], ap, reads=keys, writes=[("xpark", c - CH0)])
                    i = state["h"] % 4
                    state["h"] += 1
                    hn_, hk_ = h1n[i], ("h1n", i)
                    t = h1T[i % 2]
                    tk = [("h1T", i % 2)]
                    self.hn_prep(ap, keys + ["gbc1"], gbc1[:], hn_, hk_)

                    def later(c=c, hn_=hn_, hk_=hk_, t=t, tk=tk):
                        self.hn_trans(hn_, hk_, t[:], tk)
                        S.dma("sp", self.h1s[:, :, c * P:(c + 1) * P], t[:], reads=tk, writes=[("h1s", c)])
                    self.deferred.append(later)

                self.layer_A(l0, 0, 0, A_BLOCKS0, get_x, put_x)
                S.barrier()
            if last >= 2:
                with ExitStack() as l1:
                    self.layer_B(l1)
                    S.barrier()
            if last >= 3:
                with ExitStack() as l23:
                    X = self.sb(l23, "X", [P, NKEEP, D], F32)
                    for c in range(NKEEP):
                        S.dma("sp", X[:, c, :], self.xpark[c], writes=[("X", c)])
                    with ExitStack() as l2:
                        self.layer_C(l2, X, OWN_BLOCKS)
                        S.barrier()
                    if last >= 4:
                        with ExitStack() as l3:
                            def get_x3(c):
                                return X[:, c - CH0, :], [("X", c - CH0)]

                            def put_x3(c, ap, keys):
                                S.dma("sp", self.out[(c - 17) * P:(c - 16) * P, :], ap, reads=keys)
                            self.layer_A(l3, 3, 1, OWN_BLOCKS, get_x3, put_x3)
                            S.barrier()
                    else:
                        for c in range(17, NCH):
                            S.dma("sp", self.out[(c - 17) * P:(c - 16) * P, :], X[:, c - CH0, :], reads=[("X", c - CH0)])
            elif last == 2:
                with ExitStack() as lx:
                    xt = self.sb(lx, "xdbg", [P, D], F32)
                    for c in range(17, NCH):
                        S.dma("sp", xt[:], self.xpark[c - CH0], writes=["xdbg"])
                        S.dma("sp", self.out[(c - 17) * P:(c - 16) * P, :], xt[:], reads=["xdbg"])
            S.barrier()
            assert self.w_used == len(self.units), (self.w_used, len(self.units))
            print("instructions", S.n_ins, "waits", S.n_wait)
        return nc


def _tables(core):
    b, q = divmod(core, 4)
    t0 = q * OWN - HALO
    half = 16
    inv_freq = np.power(np.float32(500000.0), -np.arange(half, dtype=np.float32) / np.float32(half)).astype(np.float32)
    cos = np.zeros((P, NBLK, 16), np.float32)
    sin = np.zeros((P, NBLK, 16), np.float32)
    kv = np.zeros((P, NBLK), np.float32)
    for (g, r, bb), (i, start, nb, dil) in BLK.items():
        pos = t0 + start + dil * np.arange(nb)
        valid = pos >= 0
        ang = np.maximum(pos, 0).astype(np.float32)[:, None] * inv_freq[None, :]
        c, s = np.cos(ang).astype(np.float32), np.sin(ang).astype(np.float32)
        cos[:nb, i, :] = c
        sin[:nb, i, :] = s
        kv[:nb, i] = valid.astype(np.float32)
    onesv = np.zeros((P, 6, P), np.float32)
    for g in range(3):
        for bb in range(2):
            onesv[:, 2 * g + bb, :] = kv[:, BLK[(g, 0, bb)][0]][:, None]
    kk = np.arange(P)[:, None]
    qq = np.arange(P)[None, :]
    mask = np.stack([(kk >= qq), (kk <= qq)], axis=1).astype(np.float32)
    icnt = np.zeros((1, 64), np.float32)
    for gi, w in enumerate((2, 4, 8, 16)):
        pos = q * OWN + np.arange(16)
        icnt[0, gi * 16:(gi + 1) * 16] = 1.0 / np.minimum(pos + 1, w)
    return cos, sin, kv, mask, icnt, onesv


_NC_CACHE = {}


def kernel(x, norm_gain, a_w_in, a_v_gain, a_w_s, a_b_s, a_w_out, b_w_in, b_q_gain, b_k_gain, b_w_out,
           c_w_in, c_w_grp, c_scale, c_w_out, _n_layers=4):
    f = lambda a: np.ascontiguousarray(np.asarray(a, dtype=np.float32))
    x = f(x)
    if _n_layers not in _NC_CACHE:
        _NC_CACHE[_n_layers] = Builder(_n_layers).build()
    nc = _NC_CACHE[_n_layers]
    shared = {
        "ngain": f(norm_gain),
        "a_w_in": f(a_w_in),
        "a_vg": f(np.asarray(a_v_gain).reshape(2, 16, P).transpose(0, 2, 1)),
        "a_wsT": f(np.asarray(a_w_s).transpose(0, 3, 1, 2)),
        "a_bs": f(np.asarray(a_b_s).reshape(2, 1, 1024)),
        "a_w_out": f(a_w_out),
        "b_w_in": f(np.asarray(b_w_in)[0]),
        "b_qkg": f(np.stack([np.asarray(b_q_gain)[0], np.asarray(b_k_gain)[0]], axis=1).reshape(1, 768)),
        "b_w_out": f(np.asarray(b_w_out)[0]),
        "c_w_in": f(np.asarray(c_w_in)[0]),
        "c_w_grp": f(np.asarray(c_w_grp)[0]),
        "c_sc": f(np.asarray(c_scale)[0].reshape(16, P).T),
        "c_w_out": f(np.asarray(c_w_out)[0]),
        "t_ident": np.eye(P, dtype=np.float32),
    }
    in_maps = []
    for core in range(NCORE):
        b, q = divmod(core, 4)
        xe = np.zeros((NTOK, D), np.float32)
        lo = q * OWN - HALO
        src_lo = max(lo, 0)
        xe[src_lo - lo:] = x[b, src_lo:q * OWN + OWN]
        cos, sin, kv, mask, icnt, onesv = _tables(core)
        m = dict(shared)
        m.update({"xe": xe.reshape(NCH, P, D), "t_cos": cos, "t_sin": sin, "t_kv": kv, "t_mask": mask, "t_icnt": icnt, "t_ones": onesv})
        in_maps.append(m)
    res = run_bass_kernel_spmd(nc, in_maps, core_ids=list(range(NCORE)))
    out = np.zeros((2, SEQ, D), np.float32)
    for core in range(NCORE):
        b, q = divmod(core, 4)
        out[b, q * OWN:(q + 1) * OWN] = res.results[core]["out"]
    return out
```

```python
import numpy as np
import ml_dtypes
import concourse.bass as bass
import concourse.mybir as mybir
from concourse.bass_utils import run_bass_kernel_spmd

F32 = mybir.dt.float32
BF16 = mybir.dt.bfloat16
AF = mybir.ActivationFunctionType
ALU = mybir.AluOpType
AX = mybir.AxisListType

P = 128
D = 1024
KC = 8
SEQ = 8192
NCORE = 8
OWN = 2048
HALO = 2176
NTOK = OWN + HALO
NCH = NTOK // P
CH0 = 16
NKEEP = NCH - CH0
EPS = 1e-6
NSLOT = 8


class Sync:
    def __init__(self, nc, stack):
        self.nc = nc
        self.h = {"pe": nc.tensor, "act": nc.scalar, "dve": nc.vector,
                  "pool": nc.gpsimd, "sp": nc.sync}
        self.sem = {}
        self.cnt = {}
        self.seen = {}
        for e in self.h:
            self.sem[e] = stack.enter_context(nc.semaphore("s_" + e))
            self.cnt[e] = 0
            self.seen[e] = {}
        self.dsem = {}
        self.dcnt = {}
        self.dnext = {}
        for q in ("sp", "pool", "act"):
            self.dnext[q] = 0
            for s in range(NSLOT):
                k = ("d", q, s)
                self.sem[k] = stack.enter_context(nc.semaphore("d_%s%d" % (q, s)))
                self.dcnt[k] = 0
        self.last_w = {}
        self.readers = {}
        self.alias = {}
        self.n_wait = 0
        self.n_ins = 0

    def _exp(self, keys):
        out = []
        for k in keys:
            if k in self.alias:
                out.extend(self.alias[k])
            else:
                out.append(k)
        return out

    def _wait(self, eng, sig):
        k, v = sig
        if v <= 0:
            return
        if self.seen[eng].get(k, 0) >= v:
            return
        self.h[eng].wait_ge(self.sem[k], v)
        self.seen[eng][k] = v
        self.n_wait += 1

    def _deps(self, eng, reads, writes):
        for r in reads:
            w = self.last_w.get(r)
            if w is not None and not (w[0] == eng and eng == "pe"):
                self._wait(eng, w)
            if r == "pT" or (isinstance(r, tuple) and r[0] == "ps"):
                for rd in self.readers.get(r, ()):
                    if rd[0] != eng:
                        self._wait(eng, rd)
        for wk in writes:
            w = self.last_w.get(wk)
            if w is not None and not (w[0] == eng and eng == "pe"):
                self._wait(eng, w)
            for rd in self.readers.get(wk, ()):
                if not (rd[0] == eng and eng == "pe"):
                    self._wait(eng, rd)

    def _record(self, sig, reads, writes):
        for r in reads:
            self.readers.setdefault(r, []).append(sig)
        for wk in writes:
            self.last_w[wk] = sig
            self.readers[wk] = []

    def op(self, eng, fn, reads=(), writes=(), signal=True):
        reads, writes = self._exp(reads), self._exp(writes)
        self._deps(eng, reads, writes)
        ins = fn(self.h[eng])
        self.n_ins += 1
        if signal:
            ins.then_inc(self.sem[eng], 1)
            self.cnt[eng] += 1
            sig = (eng, self.cnt[eng])
        else:
            sig = (eng, self.cnt[eng] + 1)
        self._record(sig, reads, writes)
        return ins

    def dma(self, q, out, in_, reads=(), writes=()):
        reads, writes = self._exp(reads), self._exp(writes)
        self._deps(q, reads, writes)
        s = self.dnext[q] % NSLOT
        self.dnext[q] += 1
        k = ("d", q, s)
        self._wait(q, (k, self.dcnt[k]))
        ins = self.h[q].dma_start(out=out, in_=in_)
        ins.then_inc(self.sem[k], 16)
        self.dcnt[k] += 16
        self.n_ins += 1
        sig = (k, self.dcnt[k])
        self._record(sig, reads, writes)
        return sig

    def wait_all(self, eng, keys):
        for k in keys:
            w = self.last_w.get(k)
            if w is not None:
                self._wait(eng, w)

    def barrier(self):
        sigs = [(e, self.cnt[e]) for e in self.h] + [(k, v) for k, v in self.dcnt.items()]
        for e in self.h:
            for s in sigs:
                if s[0] != e:
                    self._wait(e, s)
        self.last_w = {}
        self.readers = {}


GROUPS = ((1, 1920), (4, 1536), (16, 0))


def b_blocks():
    out = {}
    i = 0
    for g, (dil, base) in enumerate(GROUPS):
        m_tot = (NTOK - base) // dil
        nblk = -(-m_tot // P)
        for r in range(dil):
            for b in range(nblk):
                nb = min(P, m_tot - P * b)
                out[(g, r, b)] = (i, base + r + dil * P * b, nb, dil)
                i += 1
    return out, i


BLK, NBLK = b_blocks()


class Builder:
    def __init__(self, n_layers=4):
        self.n_layers = n_layers
        self.nc = bass.Bass("TRN2", target_bir_lowering=False)
        nc = self.nc
        di = lambda name, shape: nc.dram_tensor(name, shape, F32, kind="ExternalInput").ap()
        self.xe = di("xe", [NCH, P, D])
        self.ngain = di("ngain", [4, D])
        self.a_w_in = di("a_w_in", [2, D, 6144])
        self.a_vg = di("a_vg", [2, P, 16])
        self.a_wsT = di("a_wsT", [2, P, 8, P])
        self.a_bs = di("a_bs", [2, 1, 1024])
        self.a_w_out = di("a_w_out", [2, 2048, D])
        self.b_w_in = di("b_w_in", [D, 10240])
        self.b_qkg = di("b_qkg", [1, 768])
        self.b_w_out = di("b_w_out", [D, D])
        self.c_w_in = di("c_w_in", [D, 4096])
        self.c_w_grp = di("c_w_grp", [4, 512, 512])
        self.c_sc = di("c_sc", [P, 16])
        self.c_w_out = di("c_w_out", [2048, D])
        self.t_cos = di("t_cos", [P, NBLK, 16])
        self.t_sin = di("t_sin", [P, NBLK, 16])
        self.t_kv = di("t_kv", [P, NBLK])
        self.t_ones = di("t_ones", [P, 6, P])
        self.t_mask = di("t_mask", [P, 2, P])
        self.t_ident = di("t_ident", [P, P])
        self.t_icnt = di("t_icnt", [1, 64])
        self.out = nc.dram_tensor("out", [OWN, D], F32, kind="ExternalOutput").ap()
        self.wbf = {
            "c_w_in": nc.dram_tensor("wbf_c_in", [D, 4096], BF16, kind="Internal").ap(),
            "c_w_out": nc.dram_tensor("wbf_c_out", [2048, D], BF16, kind="Internal").ap(),
            "a_w_in1": nc.dram_tensor("wbf_a_in1", [D, 6144], BF16, kind="Internal").ap(),
            "a_w_out1": nc.dram_tensor("wbf_a_out1", [2048, D], BF16, kind="Internal").ap(),
        }
        self.h1s = nc.dram_tensor("h1s", [P, KC, NTOK], BF16, kind="Internal").ap()
        self.xpark = nc.dram_tensor("xpark", [NKEEP, P, D], F32, kind="Internal").ap()

    def sb(self, st, name, shape, dt):
        self.uid = getattr(self, "uid", 0) + 1
        return st.enter_context(self.nc.sbuf_tensor("sb%d_%s" % (self.uid, name), shape, dt))

    def setup_common(self, st):
        nc, S = self.nc, self.S
        self.ident = self.sb(st, "ident", [P, P], BF16)
        self.ones = self.sb(st, "ones", [P, P], BF16)
        self.nhalf = self.sb(st, "nhalf", [P, 8], F32)
        self.small = self.sb(st, "small", [P, 256], F32)
        self.small_i = 0
        self.junks = [self.sb(st, "junk%d" % i, [P, D], BF16) for i in range(3)]
        self.junk_i = 0
        self.ring = [self.sb(st, "ring%d" % i, [P, 4096], BF16) for i in range(self.R)]
        for i in range(self.R):
            S.alias[("w", i)] = [("w", i, t) for t in range(3)]
        self.mask = self.sb(st, "mask", [P, 2, P], F32)
        S.dma("sp", self.mask[:], self.t_mask, writes=["mask"])
        S.dma("pool", self.ident[:], self.t_ident, writes=["ident"])
        S.op("dve", lambda e: e.memset(self.ones[:], 1.0), writes=["ones"])
        S.op("dve", lambda e: e.memset(self.nhalf[:], -0.5), writes=["nhalf"])
        S.op("dve", lambda e: e.memset(self.small[:], 1.0), writes=[("sm", j) for j in range(256)])

    def junk(self, n):
        i = self.junk_i % len(self.junks)
        self.junk_i += 1
        return self.junks[i][:, 0:n], ("junk", i)

    def smallcol(self, n=1):
        if self.small_i + n > 256:
            self.small_i = 0
        i = self.small_i
        self.small_i += n
        return self.small[:, i:i + n], [("sm", j) for j in range(i, i + n)]

    def wplan(self, units):
        self.units = units
        self.w_issued = 0
        self.w_used = 0

    def wnext(self, tag):
        S = self.S
        i = self.w_used
        assert self.units[i][0] == tag, (i, self.units[i][0], tag)
        while self.w_issued < min(i + self.R - 1, len(self.units)):
            j = self.w_issued
            _, src, shape = self.units[j]
            n = int(np.prod(shape[1:]))
            dst = self.ring[j % self.R][:, 0:n]
            if len(shape) == 3:
                dst = dst.rearrange("p (a b) -> p a b", a=shape[1])
            elif len(shape) == 4:
                dst = dst.rearrange("p (a b c) -> p a b c", a=shape[1], b=shape[2])
            if len(shape) == 4:
                for t in range(shape[2]):
                    S.dma("pool", dst[:, :, t, :], src[:, :, t, :], writes=[("w", j % self.R, t)])
            else:
                q = "sp" if src.dtype == BF16 else "pool"
                S.dma(q, dst, src, writes=[("w", j % self.R)])
            self.w_issued += 1
        self.w_used += 1
        _, src, shape = self.units[i]
        n = int(np.prod(shape[1:]))
        t = self.ring[i % self.R][:, 0:n]
        if len(shape) == 3:
            t = t.rearrange("p (a b) -> p a b", a=shape[1])
        elif len(shape) == 4:
            t = t.rearrange("p (a b c) -> p a b c", a=shape[1], b=shape[2])
        return t, ("w", i % self.R)

    @staticmethod
    def colblock(w, c0, n=512):
        return w.rearrange("(k p) n -> p k n", p=P)[:, :, c0:c0 + n], [P, KC, n]

    def rstd_of(self, ss_ap, ss_keys, inv_n):
        S = self.S
        n = ss_ap.shape[1]
        ms, mk = self.smallcol(n)
        rs, rk = self.smallcol(n)
        S.op("dve", lambda e: e.tensor_scalar(out=ms, in0=ss_ap, scalar1=inv_n, scalar2=EPS, op0=ALU.mult, op1=ALU.add),
             reads=ss_keys, writes=mk)
        S.op("pool", lambda e: e.tensor_tensor(out=rs, in0=ms, in1=self.nhalf[:, 0:n], op=ALU.pow),
             reads=mk + ["nhalf"], writes=rk)
        return rs, rk

    def psnext(self):
        i = self.ps_i % len(self.psb)
        self.ps_i += 1
        return self.psb[i], ("ps", i)

    def hn_prep(self, x_ap, xkeys, gbc, hn, hk):
        S = self.S
        ss, sk = self.smallcol(1)
        jt, jk = self.junk(D)
        S.op("act", lambda e: e.activation(out=jt, in_=x_ap, func=AF.Square, accum_out=ss),
             reads=xkeys, writes=[jk] + sk)
        rs, rk = self.rstd_of(ss, sk, 1.0 / D)
        S.op("dve", lambda e: e.scalar_tensor_tensor(out=hn[:], in0=x_ap, scalar=rs[:, 0:1], in1=gbc, op0=ALU.mult, op1=ALU.mult),
             reads=xkeys + rk + ["gbc"], writes=[hk])

    def hn_trans(self, hn, hk, dst, dkeys):
        S = self.S
        for k in range(KC):
            S.op("pe", lambda e: e.transpose(self.pT[:, k * P:(k + 1) * P], hn[:, k * P:(k + 1) * P], self.ident[:]),
                 reads=[hk, "ident"], writes=["pT"], signal=(k == KC - 1))
        S.op("act", lambda e: e.activation(out=dst, in_=self.pT[:].rearrange("p (k t) -> p k t", k=KC), func=AF.Copy),
             reads=["pT"], writes=dkeys)

    def make_hT(self, x_ap, xkeys, gbc, dst, dkeys):
        hn = self.hn[self.hn_i % len(self.hn)]
        hk = ("hn", self.hn_i % len(self.hn))
        self.hn_i += 1
        self.hn_prep(x_ap, xkeys, gbc, hn, hk)
        self.hn_trans(hn, hk, dst, dkeys)

    def out_proj(self, yT, ykey_fn, nf, chunks_x, wtag, nparts=4):
        S = self.S
        wd = D // nparts
        for part in range(nparts):
            wo, wk = self.wnext(wtag)
            for ci, (x_ap, xkeys) in enumerate(chunks_x):
                bank, bk = self.psnext()
                for f in range(nf):
                    S.op("pe", lambda e: e.matmul(bank[:, 0:wd], lhsT=yT[:, f, ci * P:(ci + 1) * P], rhs=wo[:, f, :],
                                                  start=(f == 0), stop=(f == nf - 1)),
                         reads=[ykey_fn(f), wk], writes=[bk], signal=(f == nf - 1))
                xs = x_ap[:, part * wd:(part + 1) * wd]
                S.op("dve", lambda e: e.tensor_tensor(out=xs, in0=bank[:, 0:wd], in1=xs, op=ALU.add),
                     reads=[bk] + xkeys, writes=xkeys)

    def a_units(self, j, nblocks):
        if j == 1 and self.n_layers >= 4:
            w_in = self.wbf["a_w_in1"]
            wo = self.wbf["a_w_out1"].rearrange("(f p) n -> p f n", p=P)
        else:
            w_in = self.a_w_in[j]
            wo = self.a_w_out[j].rearrange("(f p) n -> p f n", p=P)
        u = []
        for _ in range(nblocks):
            for jv in range(4):
                u.append(("a_v",) + self.colblock(w_in, 2048 + jv * 512))
            for q in range(4):
                u.append(("a_u",) + self.colblock(w_in, q * 512))
                u.append(("a_z",) + self.colblock(w_in, 4096 + q * 512))
            for qq in range(4):
                u.append(("a_o", wo[:, :, qq * 256:(qq + 1) * 256], [P, 16, 256]))
        return u

    def layer_A(self, st, li, j, blocks, get_x, put_x):
        S = self.S
        sb = lambda name, shape, dt: self.sb(st, name, shape, dt)
        gbc = sb("a_gbc", [P, D], F32)
        S.dma("sp", gbc[:], self.ngain[li:li + 1, :].partition_broadcast(P), writes=["gbc"])
        bsb = sb("a_bsb", [P, 8, P], F32)
        S.dma("sp", bsb[:].rearrange("p g i -> p (g i)"), self.a_bs[j].partition_broadcast(P), writes=["bsb"])
        wsT = sb("a_wsT", [P, 8, P], F32)
        S.dma("sp", wsT[:], self.a_wsT[j], writes=["wsT"])
        S.op("dve", lambda e: e.tensor_tensor(out=wsT[:], in0=wsT[:], in1=self.mask[:, 1, :].unsqueeze(1).to_broadcast([P, 8, P]),
                                              op=ALU.mult), reads=["wsT", "mask"], writes=["wsT"])
        vg = sb("a_vg", [P, 16], F32)
        S.dma("sp", vg[:], self.a_vg[j], writes=["vg"])
        self.deferred = []
        hTs = [sb("a_hT%d" % i, [P, KC, 512], BF16) for i in range(2)]
        vraw = [sb("a_vraw%d" % i, [P, 2048], BF16) for i in range(4)]
        wss = [sb("a_wss%d" % i, [P, 8, P], BF16) for i in range(4)]
        yT = sb("a_yT", [P, 16, 512], BF16)
        szt = [sb("a_sz%d" % i, [P, 512], F32) for i in range(2)]
        t1t = [sb("a_t1%d" % i, [P, 512], F32) for i in range(2)]
        hnb = [sb("a_hnb%d" % i, [P, D], BF16) for i in range(4)]

        def prep(bi):
            xs_ = []
            for ci, c in enumerate(blocks[bi]):
                x_ap, xkeys = get_x(c)
                xs_.append((x_ap, xkeys))
                self.hn_prep(x_ap, xkeys, gbc[:], hnb[ci], ("hnb", ci))
            return xs_

        def trans(bi):
            for ci in range(len(blocks[bi])):
                self.hn_trans(hnb[ci], ("hnb", ci), hTs[bi % 2][:, :, ci * P:(ci + 1) * P], [("hT", bi % 2, ci)])

        xs_next = prep(0)
        trans(0)
        for bi, cl in enumerate(blocks):
            n = len(cl)
            N = n * P
            hT = hTs[bi % 2]
            xs = xs_next
            if bi + 1 < len(blocks):
                xs_next = prep(bi + 1)
            hkeys = [("hT", bi % 2, ci) for ci in range(n)]
            ssa, ssk = self.smallcol(16)
            for jv in range(4):
                wt, wk = self.wnext("a_v")
                for ci in range(n):
                    bank, bk = self.psnext()
                    for k in range(KC):
                        S.op("pe", lambda e: e.matmul(bank[:, 0:512], lhsT=hT[:, k, ci * P:(ci + 1) * P], rhs=wt[:, k, :],
                                                      start=(k == 0), stop=(k == KC - 1)),
                             reads=[hkeys[ci], wk], writes=[bk], signal=(k == KC - 1))
                    col = ci * 4 + jv
                    jt, jk = self.junk(512)
                    S.op("act", lambda e: e.activation(out=jt, in_=bank[:, 0:512], func=AF.Square,
                                                       accum_out=ssa[:, col:col + 1]),
                         reads=[bk], writes=[jk, ssk[col]])
                    S.op("dve", lambda e: e.tensor_copy(out=vraw[ci][:, jv * 512:(jv + 1) * 512], in_=bank[:, 0:512]),
                         reads=[bk], writes=[("vraw", ci, jv)])
            st4, st4k = self.smallcol(4)
            S.op("dve", lambda e: e.tensor_reduce(out=st4[:, 0:n], in_=ssa[:, 0:4 * n].rearrange("p (c j) -> p c j", j=4), axis=AX.X, op=ALU.add),
                 reads=ssk[0:4 * n], writes=st4k[0:n])
            rs4, rs4k = self.rstd_of(st4[:, 0:n], st4k[0:n], 1.0 / 2048)
            for ci in range(n):
                S.op("dve", lambda e: e.tensor_scalar(out=wss[ci][:], in0=wsT[:], scalar1=rs4[:, ci:ci + 1], scalar2=None, op0=ALU.mult),
                     reads=["wsT"] + rs4k, writes=[("wss", ci)])
            for fn in self.deferred:
                fn()
            self.deferred = []
            for q in range(4):
                wu, wuk = self.wnext("a_u")
                wz, wzk = self.wnext("a_z")
                for fi in range(4):
                    f = 4 * q + fi
                    g = f // 2
                    pu, puk = self.psnext()
                    pz, pzk = self.psnext()
                    pm, pmk = self.psnext()
                    for k in range(KC):
                        S.op("pe", lambda e: e.matmul(pu[:, 0:N], lhsT=wu[:, k, fi * P:(fi + 1) * P], rhs=hT[:, k, 0:N],
                                                      start=(k == 0), stop=(k == KC - 1)),
                             reads=hkeys + [wuk], writes=[puk], signal=(k == KC - 1))
                    for k in range(KC):
                        S.op("pe", lambda e: e.matmul(pz[:, 0:N], lhsT=wz[:, k, fi * P:(fi + 1) * P], rhs=hT[:, k, 0:N],
                                                      start=(k == 0), stop=(k == KC - 1)),
                             reads=hkeys + [wzk], writes=[pzk], signal=(k == KC - 1))
                    for ci in range(n):
                        S.op("pe", lambda e: e.matmul(pm[:, ci * P:(ci + 1) * P], lhsT=vraw[ci][:, f * P:(f + 1) * P],
                                                      rhs=wss[ci][:, g, :], start=True, stop=True),
                             reads=[("vraw", ci, f // 4), ("wss", ci)], writes=[pmk], signal=(ci == n - 1))
                    sz = szt[f % 2]
                    t1 = t1t[f % 2]
                    S.op("act", lambda e: e.activation(out=sz[:, 0:N], in_=pz[:, 0:N], func=AF.Silu),
                         reads=[pzk], writes=[("sz", f % 2)])
                    S.op("dve", lambda e: e.scalar_tensor_tensor(
                        out=t1[:, 0:N].rearrange("p (c i) -> p c i", c=n), in0=pm[:, 0:N].rearrange("p (c i) -> p c i", c=n),
                        scalar=vg[:, f:f + 1], in1=bsb[:, g, :].unsqueeze(1).to_broadcast([P, n, P]),
                        op0=ALU.mult, op1=ALU.add),
                        reads=[pmk, "vg", "bsb"], writes=[("t1", f % 2)])
                    S.op("dve", lambda e: e.tensor_tensor(out=t1[:, 0:N], in0=pu[:, 0:N], in1=t1[:, 0:N], op=ALU.mult),
                         reads=[puk, ("t1", f % 2)], writes=[("t1", f % 2)])
                    S.op("dve", lambda e: e.tensor_tensor(out=yT[:, f, 0:N], in0=t1[:, 0:N], in1=sz[:, 0:N], op=ALU.mult),
                         reads=[("t1", f % 2), ("sz", f % 2)], writes=[("yT", f)])
            if bi + 1 < len(blocks):
                trans(bi + 1)
            self.out_proj(yT, lambda f: ("yT", f), 16, xs, "a_o")
            for ci, c in enumerate(cl):
                put_x(c, xs[ci][0], xs[ci][1])
        for fn in self.deferred:
            fn()
        self.deferred = []

    def b_units(self):
        w = self.b_w_in.rearrange("(k p) (j n) -> p k j n", p=P, n=P)
        u = []
        for h in range(8):
            for g in range(3):
                j0 = g * 8 + h
                u.append(("b_qkv", w[:, :, j0:j0 + 49:24, :], [P, KC, 3, P]))
            u.append(("b_z", w[:, :, 72 + h, :], [P, KC, P]))
        wo = self.b_w_out.rearrange("(f p) n -> p f n", p=P)
        for half in range(2):
            u.append(("b_o", wo[:, :, half * 512:(half + 1) * 512], [P, 8, 512]))
        return u

    def layer_B(self, st):
        S = self.S
        sb = lambda name, shape, dt: self.sb(st, name, shape, dt)
        NQ = NTOK - 2048
        hTB = sb("b_hT", [P, KC, NTOK], BF16)
        for k in range(KC):
            S.dma("sp", hTB[:, k, :], self.h1s[:, k, :], writes=["hTB"])
        cos = sb("b_cos", [P, NBLK, 16], F32)
        sin = sb("b_sin", [P, NBLK, 16], F32)
        kv = sb("b_kv", [P, NBLK], F32)
        S.dma("sp", cos[:], self.t_cos, writes=["cos"])
        S.dma("sp", sin[:], self.t_sin, writes=["sin"])
        S.dma("sp", kv[:], self.t_kv, writes=["kv"])
        onesv = sb("b_onesv", [P, 6, P], BF16)
        S.dma("pool", onesv[:], self.t_ones, writes=["onesv"])
        qkg = sb("b_qkg", [P, 3, 2, P], F32)
        S.dma("sp", qkg[:].rearrange("p g j d -> p (g j d)"), self.b_qkg.partition_broadcast(P), writes=["qkg"])
        S.op("dve", lambda e: e.tensor_scalar(out=qkg[:], in0=qkg[:], scalar1=float(np.sqrt(128.0)), scalar2=None, op0=ALU.mult),
             reads=["qkg"], writes=["qkg"])
        eps128 = sb("b_eps", [P, 8], F32)
        S.op("dve", lambda e: e.memset(eps128[:], 128.0 * EPS), writes=["eps128"])
        maskb = sb("b_maskb", [P, 2, P], BF16)
        S.op("dve", lambda e: e.tensor_copy(out=maskb[:], in_=self.mask[:]), reads=["mask"], writes=["maskb"])
        OD = sb("b_OD", [P, 2, NQ], F32)
        yTB = sb("b_yT", [P, 8, NQ], BF16)
        NQK, NRT, NV, NT, NE = 8, 3, 12, 5, 5
        qkall = sb("b_qkall", [P, NQK, 2, P], BF16)
        S.op("dve", lambda e: e.memset(qkall[:], 0.0), writes=[(("qk", i), j) for i in range(NQK) for j in range(2)])
        rtmp = [sb("b_rt%d" % i, [P, 4, 2, 2, 16], F32) for i in range(NRT)]
        pairst = {}
        Vt = [sb("b_V%d" % i, [P, P], BF16) for i in range(NV)]
        qkT = [sb("b_qkT%d" % i, [P, 2, P], BF16) for i in range(NT)]
        Et = [sb("b_E%d" % i, [P, 2, P], BF16) for i in range(NE)]
        PTt = [sb("b_PT%d" % i, [P, 2, P], BF16) for i in range(NE)]
        sqj = [sb("b_sq%d" % i, [P, P], BF16) for i in range(4)]
        szt = [sb("b_sz%d" % i, [P, 512], F32) for i in range(2)]
        xp = [sb("b_xp%d" % i, [P, D], F32) for i in range(2)]
        scale = 1.0 / float(np.sqrt(128.0))
        cnt = {"s": 0, "o": 0, "sq": 0}
        wcur = {}

        def stage_P(bs, n):
            nb, start, dil = bs["nb"], bs["start"], bs["dil"]
            bank, bk = self.psb[n % 4], ("ps", n % 4)
            bs["bank"], bs["bk"] = bank, bk
            if bs["newg"]:
                wcur["w"] = self.wnext("b_qkv")
            wq, wqk = wcur["w"]
            stop_tok = start + dil * (nb - 1) + 1
            j0 = bs["j0"]
            for k in range(KC):
                S.op("pe", lambda e: e.matmul(bank[:nb, j0 * P:384], lhsT=hTB[:, k, start:stop_tok:dil],
                                              rhs=wq[:, k, j0:3, :].rearrange("p a b -> p (a b)"),
                                              start=(k == 0), stop=(k == KC - 1)),
                     reads=["hTB", wqk], writes=[bk], signal=(k == KC - 1))

        def stage_E(bs, n):
            nb, idx, g = bs["nb"], bs["idx"], bs["g"]
            bank, bk = bs["bank"], bs["bk"]
            if n % 2 == 0:
                pairst["ss"] = self.smallcol(4)
            ssa, ska = pairst["ss"]
            o = 2 * (n % 2)
            ss, sk = ssa[:, o:o + 2], ska[o:o + 2]
            for j in range(bs["j0"], 2):
                jt = sqj[cnt["sq"] % 4]
                jk = ("sqj", cnt["sq"] % 4)
                cnt["sq"] += 1
                S.op("act", lambda e: e.activation(out=jt[:nb, :], in_=bank[:nb, j * P:(j + 1) * P], func=AF.Square,
                                                   accum_out=ss[:nb, j:j + 1]),
                     reads=[bk], writes=[jk, sk[j]])
            V = Vt[n % NV]
            Vk = ("V", n % NV)
            S.op("act", lambda e: e.activation(out=V[:nb, :], in_=bank[:nb, 2 * P:3 * P], func=AF.Copy, scale=kv[:nb, idx:idx + 1]),
                 reads=[bk, "kv"], writes=[Vk])
            bs.update(V=V, Vk=Vk, ssa=ssa, ska=ska, o=o)

        def stage_E2(blks, n0, pi):
            ssa, ska = blks[0]["ssa"], blks[0]["ska"]
            ms, mk = self.smallcol(4)
            rs, rk = self.smallcol(4)
            S.op("pool", lambda e: e.tensor_tensor(out=ms, in0=ssa, in1=eps128[:, 0:4], op=ALU.add), reads=ska + ["eps128"], writes=mk)
            S.op("pool", lambda e: e.tensor_tensor(out=rs, in0=ms, in1=self.nhalf[:, 0:4], op=ALU.pow), reads=mk + ["nhalf"], writes=rk)
            for i, bs in enumerate(blks):
                n = n0 + i
                nb, g, bank, bk, o = bs["nb"], bs["g"], bs["bank"], bs["bk"], bs["o"]
                qk = qkall[:, n % NQK, :, :]
                qkk = ("qk", n % NQK)
                for j in range(bs["j0"], 2):
                    S.op("dve", lambda e: e.scalar_tensor_tensor(out=qk[:nb, j, :], in0=bank[:nb, j * P:(j + 1) * P],
                                                                  scalar=rs[:nb, o + j:o + j + 1], in1=qkg[:nb, g, j, :],
                                                                  op0=ALU.mult, op1=ALU.mult),
                         reads=[bk] + rk + ["qkg"], writes=[(qkk, j)])
                bs.update(qk=qk, qkk=qkk)
            reng = "dve"
            rt = rtmp[pi % NRT]
            rtk = ("rt", pi % NRT)
            if len(blks) == 2 and blks[0]["nb"] == P and blks[1]["nb"] == P:
                groups = [(blks, qkall[:, n0 % NQK:n0 % NQK + 2, :, :], P, 2)]
            else:
                groups = [([bs], qkall[:, (n0 + i) % NQK:(n0 + i) % NQK + 1, :, :], bs["nb"], 1) for i, bs in enumerate(blks)]
            for bl, qv, nb, m in groups:
                idx = bl[0]["idx"]
                keys = [(b_["qkk"], j) for b_ in bl for j in range(2)]
                x1 = qv[:nb, :, :, 0:16]
                x2 = qv[:nb, :, :, 16:32]
                cb = cos[:nb, idx:idx + m, :].unsqueeze(2).to_broadcast([nb, m, 2, 16])
                sbb = sin[:nb, idx:idx + m, :].unsqueeze(2).to_broadcast([nb, m, 2, 16])
                tv = lambda t: rt[:nb, t, 0:m, :, :]
                for t, (a_, b_) in enumerate(((x1, cb), (x2, sbb), (x2, cb), (x1, sbb))):
                    S.op(reng, lambda e: e.tensor_tensor(out=tv(t), in0=a_, in1=b_, op=ALU.mult),
                         reads=keys + ["cos", "sin"], writes=[(rtk, t)])
                S.op(reng, lambda e: e.tensor_tensor(out=x1, in0=tv(0), in1=tv(1), op=ALU.subtract),
                     reads=[(rtk, 0), (rtk, 1)], writes=keys)
                S.op(reng, lambda e: e.tensor_tensor(out=x2, in0=tv(2), in1=tv(3), op=ALU.add),
                     reads=[(rtk, 2), (rtk, 3)], writes=keys)

        def stage_T(bs, n):
            nb, qk, qkk = bs["nb"], bs["qk"], bs["qkk"]
            j0 = bs["j0"]
            for j in range(j0, 2):
                S.op("pe", lambda e: e.transpose(self.pT[:, j * P:j * P + nb], qk[:nb, j, :], self.ident[:nb, :nb]),
                     reads=[(qkk, 0), (qkk, 1), "ident"], writes=["pT"], signal=(j == 1))
            T = qkT[n % NT]
            Tk = ("qkT", n % NT)
            S.op("act", lambda e: e.activation(out=T[:, j0:2, 0:nb], in_=self.pT[:, 0:2 * P].rearrange("p (j t) -> p j t", j=2)[:, j0:2, 0:nb],
                                               func=AF.Copy),
                 reads=["pT"], writes=[Tk])
            bs.update(T=T, Tk=Tk)

        def stage_S(cur, prev):
            nq = cur["nb"]
            si = cnt["s"]
            cnt["s"] += 1
            sbank, sbk = self.psb[4], ("ps", 4)
            for t, kb in enumerate((prev, cur)):
                nk = kb["nb"]
                S.op("pe", lambda e: e.matmul(sbank[:nk, t * P:t * P + nq], lhsT=kb["T"][:, 1, 0:nk], rhs=cur["T"][:, 0, 0:nq],
                                              start=True, stop=True),
                     reads=[kb["Tk"], cur["Tk"]], writes=[sbk], signal=(t == 1))
            E = Et[si % NE]
            PT = PTt[si % NE]
            Ek, PTk = ("E", si % NE), ("PT", si % NE)
            if nq == P:
                S.op("act", lambda e: e.activation(out=E[:].rearrange("p a b -> p (a b)"), in_=sbank[:, 0:2 * P], func=AF.Exp, scale=scale),
                     reads=[sbk], writes=[Ek])
                S.op("dve", lambda e: e.tensor_tensor(out=PT[:], in0=E[:], in1=maskb[:], op=ALU.mult),
                     reads=[Ek, "maskb"], writes=[PTk])
            else:
                for t, kb in enumerate((prev, cur)):
                    nk = kb["nb"]
                    S.op("act", lambda e: e.activation(out=E[:nk, t, 0:nq], in_=sbank[:nk, t * P:t * P + nq], func=AF.Exp, scale=scale),
                         reads=[sbk], writes=[Ek])
                    S.op("dve", lambda e: e.tensor_tensor(out=PT[:nk, t, 0:nq], in0=E[:nk, t, 0:nq], in1=self.mask[:nk, t, 0:nq], op=ALU.mult),
                         reads=[Ek, "mask"], writes=[PTk])
            cur.update(PT=PT, PTk=PTk)

        def stage_O(cur, prev, first):
            nq = cur["nb"]
            oi = cnt["o"]
            cnt["o"] += 1
            obank, obk = self.psb[5 + oi % 2], ("ps", 5 + oi % 2)
            PT, PTk = cur["PT"], cur["PTk"]
            for t, kb in enumerate((prev, cur)):
                nk = kb["nb"]
                S.op("pe", lambda e: e.matmul(obank[:, 0:nq], lhsT=kb["V"][:nk, :], rhs=PT[:nk, t, 0:nq],
                                              start=(t == 0), stop=(t == 1)),
                     reads=[kb["Vk"], PTk], writes=[obk], signal=False)
            for t, kb in enumerate((prev, cur)):
                nk = kb["nb"]
                ov = self.ones[:nk, :] if kb["b"] >= 2 else onesv[:nk, 2 * kb["g"] + kb["b"], :]
                S.op("pe", lambda e: e.matmul(obank[:, P:P + nq], lhsT=ov, rhs=PT[:nk, t, 0:nq],
                                              start=(t == 0), stop=(t == 1)),
                     reads=["ones", "onesv", PTk], writes=[obk], signal=(t == 1))
            q0 = cur["start"] - 2048
            q1 = q0 + cur["dil"] * (nq - 1) + 1
            osl = OD[:, :, q0:q1:cur["dil"]]
            src = obank[:, 0:2 * P].rearrange("p (a b) -> p a b", a=2)[:, :, 0:nq]
            if first:
                S.op("act", lambda e: e.activation(out=osl, in_=src, func=AF.Copy), reads=[obk], writes=["OD"])
            else:
                S.op("dve", lambda e: e.tensor_tensor(out=osl, in0=src, in1=osl, op=ALU.add), reads=[obk, "OD"], writes=["OD"])

        conv = []
        if self.n_layers >= 3:
            conv += [(self.wbf["c_w_in"][i * P:(i + 1) * P, :], self.c_w_in[i * P:(i + 1) * P, :]) for i in range(8)]
            conv += [(self.wbf["c_w_out"][i * P:(i + 1) * P, :], self.c_w_out[i * P:(i + 1) * P, :]) for i in range(16)]
        if self.n_layers >= 4:
            conv += [(self.wbf["a_w_in1"][i * P:(i + 1) * P, :], self.a_w_in[1][i * P:(i + 1) * P, :]) for i in range(8)]
            conv += [(self.wbf["a_w_out1"][i * P:(i + 1) * P, :], self.a_w_out[1][i * P:(i + 1) * P, :]) for i in range(16)]
        for h in range(8):
            n_c = -(-len(conv) // 8)
            for ci_, (dst_, src_) in enumerate(conv[h * n_c:(h + 1) * n_c]):
                S.dma("pool", dst_, src_, writes=[("wconv", h, ci_)])
            blist = []
            for g in range(3):
                dil, base = GROUPS[g]
                nblk = -(-((NTOK - base) // dil) // P)
                for r in range(dil):
                    for b in range(nblk):
                        idx, start, nb, _ = BLK[(g, r, b)]
                        blist.append(dict(g=g, r=r, b=b, idx=idx, start=start, nb=nb, dil=dil, newg=(r == 0 and b == 0),
                                           j0=(1 if b == 0 else 0)))
            NB_ = len(blist)
            npair = 0
            for n in range(NB_ + 7):
                if n < NB_:
                    stage_P(blist[n], n)
                    stage_E(blist[n], n)
                if 0 <= n - 4 < NB_:
                    stage_T(blist[n - 4], n - 4)
                if 0 <= n - 5 < NB_ and blist[n - 5]["b"] >= 1:
                    stage_S(blist[n - 5], blist[n - 6])
                if 0 <= n - 7 < NB_ and blist[n - 7]["b"] >= 1:
                    stage_O(blist[n - 7], blist[n - 8], first=(blist[n - 7]["g"] == 0))
                if n < NB_ and n % 2 == 1:
                    stage_E2(blist[n - 1:n + 1], n - 1, npair)
                    npair += 1
                elif n == NB_ - 1:
                    stage_E2(blist[n:n + 1], n, npair)
                    npair += 1
            wz, wzk = self.wnext("b_z")
            Den = OD[:, 1, :]
            Oacc = OD[:, 0, :]
            S.op("dve", lambda e: e.tensor_scalar(out=Den, in0=Den, scalar1=1e-30, scalar2=None, op0=ALU.max),
                 reads=["OD"], writes=["OD"])
            S.op("dve", lambda e: e.reciprocal(out=Den, in_=Den), reads=["OD"], writes=["OD"])
            S.op("dve", lambda e: e.tensor_tensor(out=Oacc, in0=Oacc, in1=Den, op=ALU.mult),
                 reads=["OD"], writes=["OD"])
            for tb in range(5):
                n = min(512, NQ - tb * 512)
                zb, zbk = self.psnext()
                for k in range(KC):
                    S.op("pe", lambda e: e.matmul(zb[:, 0:n], lhsT=wz[:, k, :], rhs=hTB[:, k, 2048 + tb * 512:2048 + tb * 512 + n],
                                                  start=(k == 0), stop=(k == KC - 1)),
                         reads=["hTB", wzk], writes=[zbk], signal=(k == KC - 1))
                sz = szt[tb % 2]
                S.op("act", lambda e: e.activation(out=sz[:, 0:n], in_=zb[:, 0:n], func=AF.Silu), reads=[zbk], writes=[("sz", tb % 2)])
                S.op("dve", lambda e: e.tensor_tensor(out=yTB[:, h, tb * 512:tb * 512 + n], in0=OD[:, 0, tb * 512:tb * 512 + n],
                                                      in1=sz[:, 0:n], op=ALU.mult),
                     reads=["OD", ("sz", tb % 2)], writes=[("yTB", h)])
        wo0, wo0k = self.wnext("b_o")
        wo1, wo1k = self.wnext("b_o")
        for ci in range(NKEEP):
            x = xp[ci % 2]
            xk = ("xp", ci % 2)
            S.dma("sp", x[:], self.xpark[ci], reads=[("xpark", ci)], writes=[xk])
            for half, (wo, wok) in enumerate(((wo0, wo0k), (wo1, wo1k))):
                bank, bk = self.psnext()
                for hh in range(8):
                    S.op("pe", lambda e: e.matmul(bank[:, 0:512], lhsT=yTB[:, hh, ci * P:(ci + 1) * P], rhs=wo[:, hh, :],
                                                  start=(hh == 0), stop=(hh == 7)),
                         reads=[("yTB", hh), wok], writes=[bk], signal=(hh == 7))
                xs = x[:, half * 512:(half + 1) * 512]
                S.op("dve", lambda e: e.tensor_tensor(out=xs, in0=bank[:, 0:512], in1=xs, op=ALU.add), reads=[bk, xk], writes=[xk])
            S.dma("sp", self.xpark[ci], x[:], reads=[xk], writes=[("xpark", ci)])

    def c_units(self):
        w_in = self.wbf["c_w_in"]
        wo = self.wbf["c_w_out"].rearrange("(f p) n -> p f n", p=P)
        u = []
        for g in range(4):
            u.append(("c_x",) + self.colblock(w_in, g * 512))
        for _ in range(4):
            for g in range(4):
                u.append(("c_x",) + self.colblock(w_in, g * 512))
                u.append(("c_z",) + self.colblock(w_in, 2048 + g * 512))
            for qq in range(4):
                u.append(("c_o", wo[:, :, qq * 256:(qq + 1) * 256], [P, 16, 256]))
        return u

    def layer_C(self, st, X, blocks):
        S = self.S
        sb = lambda name, shape, dt: self.sb(st, name, shape, dt)
        gbc = sb("c_gbc", [P, D], F32)
        S.dma("sp", gbc[:], self.ngain[2:3, :].partition_broadcast(P), writes=["gbc"])
        csc = sb("c_sc", [P, 16], F32)
        S.dma("sp", csc[:], self.c_sc, writes=["csc"])
        icnt = sb("c_icnt", [P, 4, 16], F32)
        S.dma("sp", icnt[:].rearrange("p g i -> p (g i)"), self.t_icnt.partition_broadcast(P), writes=["icnt"])
        wgrp = sb("c_wgrp", [P, 4, 4, 512], BF16)
        for g in range(4):
            S.dma("pool", wgrp[:, g, :, :], self.c_w_grp[g].rearrange("(fi p) n -> p fi n", p=P), writes=[("wgrp", g)])
        self.hn = [sb("c_hn%d" % i, [P, D], BF16) for i in range(2)]
        self.hn_i = 0
        hTs = [sb("c_hT%d" % i, [P, KC, 512], BF16) for i in range(2)]
        yT = sb("c_yT", [P, 16, 512], BF16)
        dT = [sb("c_dT%d" % i, [P, 512], BF16) for i in range(4)]
        xcb = [sb("c_xcb%d" % i, [P, 528], F32) for i in range(2)]
        sab = [sb("c_sab%d" % i, [P, 528], F32) for i in range(2)]
        carry = sb("c_carry", [P, 16, 16], F32)
        fix = sb("c_fix", [P, 16], F32)
        szt = [sb("c_sz%d" % i, [P, 512], F32) for i in range(2)]
        hT16 = hTs[1][:, :, 0:P]
        self.make_hT(X[:, 0, :], [("X", 0)], gbc[:], hT16, [("hT", 1, 0)])
        for g in range(4):
            wx, wxk = self.wnext("c_x")
            for fi in range(4):
                f = 4 * g + fi
                bank, bk = self.psnext()
                for k in range(KC):
                    S.op("pe", lambda e: e.matmul(bank[:, 0:16], lhsT=wx[:, k, fi * P:(fi + 1) * P], rhs=hT16[:, k, 112:128],
                                                  start=(k == 0), stop=(k == KC - 1)),
                         reads=[("hT", 1, 0), wxk], writes=[bk], signal=(k == KC - 1))
                S.op("act", lambda e: e.activation(out=carry[:, f, :], in_=bank[:, 0:16], func=AF.Copy),
                     reads=[bk], writes=[("carry", f)])
        xi = 0
        hnb = [sb("c_hnb%d" % i, [P, D], BF16) for i in range(4)]

        def prep(bi):
            xs_ = []
            for ci, c in enumerate(blocks[bi]):
                x_ap, xkeys = X[:, c - CH0, :], [("X", c - CH0)]
                xs_.append((x_ap, xkeys))
                self.hn_prep(x_ap, xkeys, gbc[:], hnb[ci], ("hnb", ci))
            return xs_

        def trans(bi):
            for ci in range(len(blocks[bi])):
                self.hn_trans(hnb[ci], ("hnb", ci), hTs[bi % 2][:, :, ci * P:(ci + 1) * P], [("hT", bi % 2, ci)])

        xs_next = prep(0)
        trans(0)
        for bi, cl in enumerate(blocks):
            n = len(cl)
            N = n * P
            hT = hTs[bi % 2]
            xs = xs_next
            if bi + 1 < len(blocks):
                xs_next = prep(bi + 1)
            hkeys = [("hT", bi % 2, ci) for ci in range(n)]
            for g in range(4):
                w = 2 << g
                wx, wxk = self.wnext("c_x")
                wz, wzk = self.wnext("c_z")
                for fi in range(4):
                    f = 4 * g + fi
                    px, pxk = self.psnext()
                    for k in range(KC):
                        S.op("pe", lambda e: e.matmul(px[:, 0:N], lhsT=wx[:, k, fi * P:(fi + 1) * P], rhs=hT[:, k, 0:N],
                                                      start=(k == 0), stop=(k == KC - 1)),
                             reads=hkeys + [wxk], writes=[pxk], signal=(k == KC - 1))
                    xb = xcb[xi % 2]
                    xk = ("xcb", xi % 2)
                    xi += 1
                    S.op("act", lambda e: e.activation(out=xb[:, 16:16 + N], in_=px[:, 0:N], func=AF.Copy),
                         reads=[pxk], writes=[xk])
                    S.op("act", lambda e: e.activation(out=xb[:, 0:16], in_=carry[:, f, :], func=AF.Copy),
                         reads=[("carry", f)], writes=[xk])
                    S.op("act", lambda e: e.activation(out=carry[:, f, :], in_=xb[:, N:N + 16], func=AF.Copy),
                         reads=[xk], writes=[("carry", f)])
                    cur, ck = xb, xk
                    step = 1
                    si = 0
                    while step < w:
                        i0 = 2 * step - 1
                        nxt, nk = sab[si % 2], ("sab", si % 2)
                        si += 1
                        S.op("dve", lambda e: e.tensor_tensor(out=nxt[:, i0:16 + N], in0=cur[:, i0:16 + N],
                                                              in1=cur[:, i0 - step:16 + N - step], op=ALU.add),
                             reads=[ck], writes=[nk])
                        cur, ck = nxt, nk
                        step *= 2
                    S.op("dve", lambda e: e.scalar_tensor_tensor(out=dT[fi][:, 0:N], in0=cur[:, 16:16 + N], scalar=1.0 / w,
                                                                  in1=xb[:, 16:16 + N], op0=ALU.mult, op1=ALU.subtract),
                         reads=[ck, xk], writes=[("dT", fi)])
                    if bi == 0:
                        S.op("dve", lambda e: e.tensor_tensor(out=fix[:], in0=cur[:, 16:32], in1=icnt[:, g, :], op=ALU.mult),
                             reads=[ck, "icnt"], writes=["fix"])
                        S.op("dve", lambda e: e.tensor_tensor(out=dT[fi][:, 0:16], in0=fix[:], in1=xb[:, 16:32], op=ALU.subtract),
                             reads=["fix", xk, ("dT", fi)], writes=[("dT", fi)])
                for fo in range(4):
                    f = 4 * g + fo
                    pm, pmk = self.psnext()
                    pz, pzk = self.psnext()
                    for fi in range(4):
                        S.op("pe", lambda e: e.matmul(pm[:, 0:N], lhsT=wgrp[:, g, fi, fo * P:(fo + 1) * P], rhs=dT[fi][:, 0:N],
                                                      start=(fi == 0), stop=(fi == 3)),
                             reads=[("wgrp", g), ("dT", fi)], writes=[pmk], signal=(fi == 3))
                    for k in range(KC):
                        S.op("pe", lambda e: e.matmul(pz[:, 0:N], lhsT=wz[:, k, fo * P:(fo + 1) * P], rhs=hT[:, k, 0:N],
                                                      start=(k == 0), stop=(k == KC - 1)),
                             reads=hkeys + [wzk], writes=[pzk], signal=(k == KC - 1))
                    sz = szt[f % 2]
                    S.op("act", lambda e: e.activation(out=sz[:, 0:N], in_=pz[:, 0:N], func=AF.Silu),
                         reads=[pzk], writes=[("sz", f % 2)])
                    S.op("dve", lambda e: e.scalar_tensor_tensor(out=yT[:, f, 0:N], in0=pm[:, 0:N], scalar=csc[:, f:f + 1],
                                                                  in1=sz[:, 0:N], op0=ALU.mult, op1=ALU.mult),
                         reads=[pmk, "csc", ("sz", f % 2)], writes=[("yT", f)])
            if bi + 1 < len(blocks):
                trans(bi + 1)
            self.out_proj(yT, lambda f: ("yT", f), 16, xs, "c_o")

    def build(self):
        from contextlib import ExitStack
        nc = self.nc
        self.R = 4
        A_BLOCKS0 = [[0, 1, 2, 3], [4, 5, 6, 7], [8, 9, 10, 11], [12, 13, 14, 15], [16],
                     [17, 18, 19, 20], [21, 22, 23, 24], [25, 26, 27, 28], [29, 30, 31, 32]]
        OWN_BLOCKS = A_BLOCKS0[5:]
        units = self.a_units(0, len(A_BLOCKS0))
        if self.n_layers >= 2:
            units += self.b_units()
        if self.n_layers >= 3:
            units += self.c_units()
        if self.n_layers >= 4:
            units += self.a_units(1, len(OWN_BLOCKS))
        self.wplan(units)
        with ExitStack() as st:
            self.S = S = Sync(nc, st)
            self.st0 = st
            self.setup_common(st)
            self.psb = [st.enter_context(nc.psum_tensor("psb%d" % i, [P, 512], F32)) for i in range(7)]
            self.ps_i = 0
            self.pT = st.enter_context(nc.psum_tensor("pT", [P, 1024], BF16))
            last = self.n_layers
            with ExitStack() as l0:
                xin = [self.sb(l0, "xin%d" % i, [P, D], F32) for i in range(8)]
                gbc1 = self.sb(l0, "gbc1", [P, D], F32)
                S.dma("sp", gbc1[:], self.ngain[1:2, :].partition_broadcast(P), writes=["gbc1"])
                h1T = [self.sb(l0, "h1T%d" % i, [P, KC, P], BF16) for i in range(2)]
                state = {"i": 0, "slot": {}, "h": 0}
                h1n = [self.sb(l0, "h1n%d" % i, [P, D], BF16) for i in range(4)]

                def get_x(c):
                    s = state["i"] % 8
                    state["i"] += 1
                    state["slot"][c] = s
                    S.dma("sp", xin[s][:], self.xe[c], writes=[("xin", s)])
                    return xin[s][:], [("xin", s)]

                def put_x(c, ap, keys):
                    if last == 1:
                        if c >= 17:
                            S.dma("sp", self.out[(c - 17) * P:(c - 16) * P, :], ap, reads=keys)
                        return
                    if c >= CH0:
                        S.dma("sp", self.xpark[c - CH0], ap, reads=keys, writes=[("xpark", c - CH0)])
                    i = state["h"] % 4
                    state["h"] += 1
                    hn_, hk_ = h1n[i], ("h1n", i)
                    t = h1T[i % 2]
                    tk = [("h1T", i % 2)]
                    self.hn_prep(ap, keys + ["gbc1"], gbc1[:], hn_, hk_)

                    def later(c=c, hn_=hn_, hk_=hk_, t=t, tk=tk):
                        self.hn_trans(hn_, hk_, t[:], tk)
                        S.dma("sp", self.h1s[:, :, c * P:(c + 1) * P], t[:], reads=tk, writes=[("h1s", c)])
                    self.deferred.append(later)

                self.layer_A(l0, 0, 0, A_BLOCKS0, get_x, put_x)
                S.barrier()
            if last >= 2:
                with ExitStack() as l1:
                    self.layer_B(l1)
                    S.barrier()
            if last >= 3:
                with ExitStack() as l23:
                    X = self.sb(l23, "X", [P, NKEEP, D], F32)
                    for c in range(NKEEP):
                        S.dma("sp", X[:, c, :], self.xpark[c], writes=[("X", c)])
                    with ExitStack() as l2:
                        self.layer_C(l2, X, OWN_BLOCKS)
                        S.barrier()
                    if last >= 4:
                        with ExitStack() as l3:
                            def get_x3(c):
                                return X[:, c - CH0, :], [("X", c - CH0)]

                            def put_x3(c, ap, keys):
                                S.dma("sp", self.out[(c - 17) * P:(c - 16) * P, :], ap, reads=keys)
                            self.layer_A(l3, 3, 1, OWN_BLOCKS, get_x3, put_x3)
                            S.barrier()
                    else:
                        for c in range(17, NCH):
                            S.dma("sp", self.out[(c - 17) * P:(c - 16) * P, :], X[:, c - CH0, :], reads=[("X", c - CH0)])
            elif last == 2:
                with ExitStack() as lx:
                    xt = self.sb(lx, "xdbg", [P, D], F32)
                    for c in range(17, NCH):
                        S.dma("sp", xt[:], self.xpark[c - CH0], writes=["xdbg"])
                        S.dma("sp", self.out[(c - 17) * P:(c - 16) * P, :], xt[:], reads=["xdbg"])
            S.barrier()
            assert self.w_used == len(self.units), (self.w_used, len(self.units))
            print("instructions", S.n_ins, "waits", S.n_wait)
        return nc


def _tables(core):
    b, q = divmod(core, 4)
    t0 = q * OWN - HALO
    half = 16
    inv_freq = np.power(np.float32(500000.0), -np.arange(half, dtype=np.float32) / np.float32(half)).astype(np.float32)
    cos = np.zeros((P, NBLK, 16), np.float32)
    sin = np.zeros((P, NBLK, 16), np.float32)
    kv = np.zeros((P, NBLK), np.float32)
    for (g, r, bb), (i, start, nb, dil) in BLK.items():
        pos = t0 + start + dil * np.arange(nb)
        valid = pos >= 0
        ang = np.maximum(pos, 0).astype(np.float32)[:, None] * inv_freq[None, :]
        c, s = np.cos(ang).astype(np.float32), np.sin(ang).astype(np.float32)
        cos[:nb, i, :] = c
        sin[:nb, i, :] = s
        kv[:nb, i] = valid.astype(np.float32)
    onesv = np.zeros((P, 6, P), np.float32)
    for g in range(3):
        for bb in range(2):
            onesv[:, 2 * g + bb, :] = kv[:, BLK[(g, 0, bb)][0]][:, None]
    kk = np.arange(P)[:, None]
    qq = np.arange(P)[None, :]
    mask = np.stack([(kk >= qq), (kk <= qq)], axis=1).astype(np.float32)
    icnt = np.zeros((1, 64), np.float32)
    for gi, w in enumerate((2, 4, 8, 16)):
        pos = q * OWN + np.arange(16)
        icnt[0, gi * 16:(gi + 1) * 16] = 1.0 / np.minimum(pos + 1, w)
    return cos, sin, kv, mask, icnt, onesv


_NC_CACHE = {}


def kernel(x, norm_gain, a_w_in, a_v_gain, a_w_s, a_b_s, a_w_out, b_w_in, b_q_gain, b_k_gain, b_w_out,
           c_w_in, c_w_grp, c_scale, c_w_out, _n_layers=4):
    f = lambda a: np.ascontiguousarray(np.asarray(a, dtype=np.float32))
    x = f(x)
    if _n_layers not in _NC_CACHE:
        _NC_CACHE[_n_layers] = Builder(_n_layers).build()
    nc = _NC_CACHE[_n_layers]
    shared = {
        "ngain": f(norm_gain),
        "a_w_in": f(a_w_in),
        "a_vg": f(np.asarray(a_v_gain).reshape(2, 16, P).transpose(0, 2, 1)),
        "a_wsT": f(np.asarray(a_w_s).transpose(0, 3, 1, 2)),
        "a_bs": f(np.asarray(a_b_s).reshape(2, 1, 1024)),
        "a_w_out": f(a_w_out),
        "b_w_in": f(np.asarray(b_w_in)[0]),
        "b_qkg": f(np.stack([np.asarray(b_q_gain)[0], np.asarray(b_k_gain)[0]], axis=1).reshape(1, 768)),
        "b_w_out": f(np.asarray(b_w_out)[0]),
        "c_w_in": f(np.asarray(c_w_in)[0]),
        "c_w_grp": f(np.asarray(c_w_grp)[0]),
        "c_sc": f(np.asarray(c_scale)[0].reshape(16, P).T),
        "c_w_out": f(np.asarray(c_w_out)[0]),
        "t_ident": np.eye(P, dtype=np.float32),
    }
    in_maps = []
    for core in range(NCORE):
        b, q = divmod(core, 4)
        xe = np.zeros((NTOK, D), np.float32)
        lo = q * OWN - HALO
        src_lo = max(lo, 0)
        xe[src_lo - lo:] = x[b, src_lo:q * OWN + OWN]
        cos, sin, kv, mask, icnt, onesv = _tables(core)
        m = dict(shared)
        m.update({"xe": xe.reshape(NCH, P, D), "t_cos": cos, "t_sin": sin, "t_kv": kv, "t_mask": mask, "t_icnt": icnt, "t_ones": onesv})
        in_maps.append(m)
    res = run_bass_kernel_spmd(nc, in_maps, core_ids=list(range(NCORE)))
    out = np.zeros((2, SEQ, D), np.float32)
    for core in range(NCORE):
        b, q = divmod(core, 4)
        out[b, q * OWN:(q + 1) * OWN] = res.results[core]["out"]
    return out
```
